# Optimizing a Trainium2 kernel written in Bass

```python
import jax
import jax.numpy as jnp
from jax import lax
import numpy as np

D_MODEL = 1024
BATCH = 8
SEQ = 2048
DEPTH = 1
DEC_BATCH = 128
DEC_SEQ = 1
PAST_LEN = 16384
PAGE_SIZE = 128

MIX_WIDTH = D_MODEL
POOL_WIDTH = MIX_WIDTH // 2
POOL_WINDOWS = (2, 4, 8, 16)
POOL_GROUPS = len(POOL_WINDOWS)
POOL_GROUP_WIDTH = POOL_WIDTH // POOL_GROUPS
POOL_BUF = max(POOL_WINDOWS) - 1
GLA_WIDTH = MIX_WIDTH - POOL_WIDTH
GLA_HEADS = 4
GLA_DV = GLA_WIDTH // GLA_HEADS
GLA_DK = GLA_DV // 2
GLA_KEY_WIDTH = GLA_HEADS * GLA_DK
GLA_GATE_RANK = 16
GLA_GATE_NORM = 16.0
GLA_CHUNK = 64
MEM_TOKENS = 256
MEM_HEADS = 4
MEM_HEAD_DIM = D_MODEL // MEM_HEADS
D_FF = 4 * D_MODEL
EPS = 1e-6
SPLITS = (POOL_WIDTH, POOL_WIDTH + GLA_KEY_WIDTH, POOL_WIDTH + 2 * GLA_KEY_WIDTH,
          POOL_WIDTH + 2 * GLA_KEY_WIDTH + GLA_WIDTH, POOL_WIDTH + 2 * GLA_KEY_WIDTH + 2 * GLA_WIDTH)
IN_COLS = POOL_WIDTH + 2 * GLA_KEY_WIDTH + 2 * GLA_WIDTH + GLA_GATE_RANK

kernel_name = 'hybrid_pool_gla_memxattn_step'


def _rmsnorm(x, g):
    xf = x.astype(jnp.float32)
    r = xf * lax.rsqrt(jnp.mean(xf * xf, axis=-1, keepdims=True) + EPS)
    return (r * g.astype(jnp.float32)).astype(x.dtype)


def _pool_mixer(u_ext, n_prefix, w_pool, pool_scale):
    b, length, _ = u_ext.shape
    uf = u_ext.astype(jnp.float32)
    cs = jnp.concatenate([jnp.zeros((b, 1, POOL_WIDTH), jnp.float32), jnp.cumsum(uf, axis=1)], axis=1)
    idx = jnp.arange(n_prefix, length)
    u_new = uf[:, n_prefix:]
    outs = []
    for gi, w in enumerate(POOL_WINDOWS):
        c0 = gi * POOL_GROUP_WIDTH
        c1 = c0 + POOL_GROUP_WIDTH
        lo = jnp.maximum(idx + 1 - w, 0)
        win_sum = cs[:, idx + 1, c0:c1] - cs[:, lo, c0:c1]
        count = (idx + 1 - lo).astype(jnp.float32)[None, :, None]
        pooled = win_sum / count - u_new[:, :, c0:c1]
        outs.append(jnp.einsum('btc,cd->btd', pooled, w_pool[gi].astype(jnp.float32)))
    y = jnp.concatenate(outs, axis=-1) * pool_scale.astype(jnp.float32)
    return y.astype(u_ext.dtype)


def _to_chunks(a, n_chunks, chunk):
    b, t, h, d = a.shape
    pad = n_chunks * chunk - t
    a = jnp.pad(a.astype(jnp.float32), ((0, 0), (0, pad), (0, 0), (0, 0)))
    a = a.reshape(b, n_chunks, chunk, h, d)
    return jnp.transpose(a, (1, 0, 3, 2, 4))


def _gla(q, k, v, log_f, s0):
    b, t = q.shape[:2]
    chunk = min(GLA_CHUNK, t)
    n_chunks = -(-t // chunk)
    qc, kc, vc, gc = (_to_chunks(a, n_chunks, chunk) for a in (q, k, v, log_f))
    cum = jnp.cumsum(gc, axis=3)
    cum_last = cum[:, :, :, -1:, :]
    q_dec = qc * jnp.exp(cum)
    k_inv = kc * jnp.exp(-cum)
    k_end = kc * jnp.exp(cum_last - cum)
    causal = jnp.tril(jnp.ones((chunk, chunk), jnp.float32))
    scores = jnp.einsum('nbhid,nbhjd->nbhij', q_dec, k_inv) * causal
    o_intra = jnp.einsum('nbhij,nbhjv->nbhiv', scores, vc)

    def step(state, inp):
        q_c, k_c, v_c, last_c = inp
        o_c = jnp.einsum('bhid,bhdv->bhiv', q_c, state)
        state = jnp.exp(last_c[:, :, 0, :])[..., None] * state + jnp.einsum('bhjd,bhjv->bhdv', k_c, v_c)
        return state, o_c

    s_fin, o_inter = lax.scan(step, s0.astype(jnp.float32), (q_dec, k_end, vc, cum_last))
    o = jnp.transpose(o_intra + o_inter, (1, 0, 3, 2, 4)).reshape(b, n_chunks * chunk, GLA_HEADS, GLA_DV)
    return o[:, :t], s_fin


def _mem_kv(mem, mem_norm_g, w_km, w_vm):
    b = mem.shape[0]
    mn = _rmsnorm(mem, mem_norm_g)
    mk = jnp.einsum('bmd,dc->bmc', mn, w_km).reshape(b, MEM_TOKENS, MEM_HEADS, MEM_HEAD_DIM)
    mv = jnp.einsum('bmd,dc->bmc', mn, w_vm).reshape(b, MEM_TOKENS, MEM_HEADS, MEM_HEAD_DIM)
    return mk, mv


def _layer(x, pool_prefix, gla_state, mem_k, mem_v, norm_mix_g, w_in, w_forget_up, b_forget, w_pool,
           pool_scale, gla_norm_g, w_out, norm_mem_g, w_qm, w_om, norm_ffn_g, w_up, w_down):
    b, t, _ = x.shape
    h = _rmsnorm(x, norm_mix_g)
    proj = jnp.einsum('btd,dc->btc', h, w_in)
    u, q, k, v, gate, f_low = jnp.split(proj, SPLITS, axis=-1)
    u_ext = jnp.concatenate([pool_prefix.astype(u.dtype), u], axis=1)
    pool_out = _pool_mixer(u_ext, pool_prefix.shape[1], w_pool, pool_scale)
    new_pool = u_ext[:, -POOL_BUF:]
    log_f = jax.nn.log_sigmoid((jnp.einsum('btr,rk->btk', f_low, w_forget_up) + b_forget).astype(jnp.float32)) / GLA_GATE_NORM
    q = q.reshape(b, t, GLA_HEADS, GLA_DK) * (GLA_DK ** -0.5)
    k = k.reshape(b, t, GLA_HEADS, GLA_DK)
    v = v.reshape(b, t, GLA_HEADS, GLA_DV)
    log_f = log_f.reshape(b, t, GLA_HEADS, GLA_DK)
    o, new_gla = _gla(q, k, v, log_f, gla_state)
    o = _rmsnorm(o, gla_norm_g).reshape(b, t, GLA_WIDTH).astype(x.dtype) * jax.nn.silu(gate)
    mixed = jnp.concatenate([pool_out, o], axis=-1)
    x = x + jnp.einsum('btc,cd->btd', mixed, w_out)
    hm = _rmsnorm(x, norm_mem_g)
    qm = jnp.einsum('btd,dc->btc', hm, w_qm).reshape(b, t, MEM_HEADS, MEM_HEAD_DIM)
    s = jnp.einsum('bthd,bmhd->bhtm', qm.astype(jnp.float32), mem_k.astype(jnp.float32)) * (MEM_HEAD_DIM ** -0.5)
    p = jax.nn.softmax(s, axis=-1)
    ctx = jnp.einsum('bhtm,bmhd->bthd', p, mem_v.astype(jnp.float32)).reshape(b, t, D_MODEL).astype(x.dtype)
    x = x + jnp.einsum('btc,cd->btd', ctx, w_om)
    hf = _rmsnorm(x, norm_ffn_g)
    a = jax.nn.relu(jnp.einsum('btd,df->btf', hf, w_up))
    x = x + jnp.einsum('btf,fd->btd', a * a, w_down)
    return x, new_pool, new_gla.astype(gla_state.dtype)


def setup_inputs(seed: int = 0) -> dict:
    key = jax.random.key(seed)
    ks = jax.random.split(key, 32)
    f32 = jnp.float32

    def nrm(k, shape, scale):
        return jax.random.normal(k, shape, f32) * scale

    def gain(k, shape):
        return 1.0 + 0.02 * jax.random.normal(k, shape, f32)

    L = DEPTH
    return {
        'x_prompt': nrm(ks[0], (BATCH, SEQ, D_MODEL), 1.0),
        'x_sample': nrm(ks[1], (DEC_BATCH, DEC_SEQ, D_MODEL), 1.0),
        'state_pool': nrm(ks[2], (L, DEC_BATCH, POOL_BUF, POOL_WIDTH), 1.0),
        'state_gla': nrm(ks[3], (L, DEC_BATCH, GLA_HEADS, GLA_DK, GLA_DV), 0.5),
        'cache_mem_k': nrm(ks[4], (L, DEC_BATCH, MEM_TOKENS, MEM_HEADS, MEM_HEAD_DIM), 1.0),
        'cache_mem_v': nrm(ks[5], (L, DEC_BATCH, MEM_TOKENS, MEM_HEADS, MEM_HEAD_DIM), 1.0),
        'mem_prompt': nrm(ks[6], (BATCH, MEM_TOKENS, D_MODEL), 1.0),
        'norm_mix_g': gain(ks[7], (L, D_MODEL)),
        'w_in': nrm(ks[8], (L, D_MODEL, IN_COLS), D_MODEL ** -0.5),
        'w_forget_up': nrm(ks[9], (L, GLA_GATE_RANK, GLA_KEY_WIDTH), GLA_GATE_RANK ** -0.5),
        'b_forget': nrm(ks[10], (L, GLA_KEY_WIDTH), 0.1),
        'w_pool': nrm(ks[11], (L, POOL_GROUPS, POOL_GROUP_WIDTH, POOL_GROUP_WIDTH), POOL_GROUP_WIDTH ** -0.5),
        'pool_scale': gain(ks[12], (L, POOL_WIDTH)),
        'gla_norm_g': gain(ks[13], (L, GLA_HEADS, GLA_DV)),
        'w_out': nrm(ks[14], (L, MIX_WIDTH, D_MODEL), MIX_WIDTH ** -0.5),
        'mem_norm_g': gain(ks[15], (L, D_MODEL)),
        'w_km': nrm(ks[16], (L, D_MODEL, D_MODEL), D_MODEL ** -0.5),
        'w_vm': nrm(ks[17], (L, D_MODEL, D_MODEL), D_MODEL ** -0.5),
        'norm_mem_g': gain(ks[18], (L, D_MODEL)),
        'w_qm': nrm(ks[19], (L, D_MODEL, D_MODEL), D_MODEL ** -0.5),
        'w_om': nrm(ks[20], (L, D_MODEL, D_MODEL), D_MODEL ** -0.5),
        'norm_ffn_g': gain(ks[21], (L, D_MODEL)),
        'w_up': nrm(ks[22], (L, D_MODEL, D_FF), D_MODEL ** -0.5),
        'w_down': nrm(ks[23], (L, D_FF, D_MODEL), D_FF ** -0.5),
        'norm_final_g': gain(ks[24], (D_MODEL,)),
    }


def reference(x_prompt, x_sample, state_pool, state_gla, cache_mem_k, cache_mem_v, mem_prompt,
              norm_mix_g, w_in, w_forget_up, b_forget, w_pool, pool_scale, gla_norm_g, w_out,
              mem_norm_g, w_km, w_vm, norm_mem_g, w_qm, w_om, norm_ffn_g, w_up, w_down, norm_final_g):
    b_p = x_prompt.shape[0]
    xp = x_prompt
    xs = x_sample
    pool_p, gla_p, mk_p, mv_p, pool_s, gla_s = [], [], [], [], [], []
    for l in range(DEPTH):
        lw = (norm_mix_g[l], w_in[l], w_forget_up[l], b_forget[l], w_pool[l], pool_scale[l], gla_norm_g[l],
              w_out[l], norm_mem_g[l], w_qm[l], w_om[l], norm_ffn_g[l], w_up[l], w_down[l])
        mk, mv = _mem_kv(mem_prompt, mem_norm_g[l], w_km[l], w_vm[l])
        xp, sp, sg = _layer(xp, jnp.zeros((b_p, 0, POOL_WIDTH), xp.dtype),
                            jnp.zeros((b_p, GLA_HEADS, GLA_DK, GLA_DV), xp.dtype), mk, mv, *lw)
        pool_p.append(sp)
        gla_p.append(sg)
        mk_p.append(mk)
        mv_p.append(mv)
        xs, ss, gs = _layer(xs, state_pool[l], state_gla[l], cache_mem_k[l], cache_mem_v[l], *lw)
        pool_s.append(ss)
        gla_s.append(gs)
    y_prompt = _rmsnorm(xp, norm_final_g)
    y_sample = _rmsnorm(xs, norm_final_g)
    return (y_prompt, y_sample, jnp.stack(pool_p), jnp.stack(gla_p), jnp.stack(mk_p), jnp.stack(mv_p),
            jnp.stack(pool_s), jnp.stack(gla_s))
```

```python
import numpy as np
from contextlib import ExitStack
import concourse.bass as bass
import concourse.mybir as mybir
from concourse.bass_utils import run_bass_kernel_spmd

F32 = mybir.dt.float32
BF16 = mybir.dt.bfloat16
AF = mybir.ActivationFunctionType
ALU = mybir.AluOpType
AX = mybir.AxisListType

ENGS = ('pe', 'act', 'dve', 'pool', 'sp')
NCORES = 8
D = 1024
T = 2048
NS = 16
TT = T + NS
EPS = 1e-6
import os
STOP = 99
CUT = int(os.environ.get('K_CUT', '99'))


class Buf:
    __slots__ = ('name', 'w', 'r', 'excl')

    def __init__(self, name='', excl=False):
        self.name = name
        self.w = None
        self.r = {}
        self.excl = excl


class Op:
    __slots__ = ('eng', 'fn', 'deps', 'sig', 'dma', 'sem', 'val', 'pre', 'seq', 'inc')


class Sched:
    def __init__(self, nc):
        self.nc = nc
        self.ops = {e: [] for e in ENGS}
        self.all = []
        self.out_dmas = []
        self.last = {e: None for e in ENGS}
        self.live_dmas = []
        self.dead = False

    def _mk(self, eng, fn, dma, inc):
        op = Op()
        op.eng = eng
        op.fn = fn
        op.dma = dma
        op.sig = dma
        op.inc = inc
        op.seq = len(self.all)
        op.sem = None
        op.val = 0
        op.pre = None
        op.deps = []
        return op

    def add(self, eng, fn, reads=(), writes=(), dma=False, inc=16, is_output=False):
        if self.dead:
            return None
        op = self._mk(eng, fn, dma, inc)
        deps = {}
        for b in reads:
            d = b.w
            if d is not None:
                deps[d.seq] = d
            if b.excl:
                for d in b.r.values():
                    deps[d.seq] = d
        for b in writes:
            d = b.w
            if d is not None:
                deps[d.seq] = d
            for d in b.r.values():
                deps[d.seq] = d
        for d in deps.values():
            if d is op:
                continue
            if (not d.dma) and (not dma) and d.eng == eng and eng == 'pe':
                continue
            d.sig = True
            op.deps.append(d)
        key = ('dma', op.seq) if dma else eng
        for b in reads:
            b.r[key] = op
        for b in writes:
            b.w = op
            b.r = {}
        self.ops[eng].append(op)
        self.all.append(op)
        if dma:
            self.live_dmas.append(op)
        else:
            self.last[eng] = op
        if is_output:
            self.out_dmas.append(op)
        return op

    def barrier(self):
        if self.dead:
            return
        lasts = [o for o in self.last.values() if o is not None] + list(self.live_dmas)
        for o in lasts:
            o.sig = True
        for eng in ENGS:
            op = self._mk(eng, None, False, 0)
            op.deps = [o for o in lasts if not ((not o.dma) and o.eng == eng and eng == 'pe')]
            self.ops[eng].append(op)
            self.all.append(op)
        self.live_dmas = []

    def finish(self, eng='sp'):
        op = self._mk(eng, None, False, 0)
        op.deps = list(self.out_dmas)
        self.ops[eng].append(op)
        self.all.append(op)

    def emit(self, sems_eng, sems_dma):
        cnt = {e: 0 for e in ENGS}
        rr = {e: 0 for e in ENGS}
        cur = {}
        for op in self.all:
            if op.dma:
                lst = sems_dma[op.eng]
                s = lst[rr[op.eng] % len(lst)]
                rr[op.eng] += 1
                v = cur.get(id(s), 0)
                op.pre = (s, v) if v > 0 else None
                op.sem = s
                op.val = v + op.inc
                cur[id(s)] = op.val
            elif op.sig:
                cnt[op.eng] += 1
                op.sem = sems_eng[op.eng]
                op.val = cnt[op.eng]
        sched = self

        def run(eng_name, engine):
            waited = {}
            for op in sched.ops[eng_name]:
                ws = []
                if op.pre is not None:
                    ws.append(op.pre)
                for d in op.deps:
                    ws.append((d.sem, d.val))
                for (s, v) in ws:
                    if waited.get(id(s), 0) < v:
                        engine.wait_ge(s, v)
                        waited[id(s)] = v
                if op.fn is None:
                    continue
                ins = op.fn(engine)
                if op.sig:
                    ins.then_inc(op.sem, op.inc if op.dma else 1)

        with self.nc.Block() as block:
            @block.sync
            def _(e):
                run('sp', e)

            @block.scalar
            def _(e):
                run('act', e)

            @block.vector
            def _(e):
                run('dve', e)

            @block.gpsimd
            def _(e):
                run('pool', e)

            @block.tensor
            def _(e):
                run('pe', e)


def build_nc():
    nc = bass.Bass("TRN2", target_bir_lowering=False)

    def din(name, shape, dt=F32):
        return nc.dram_tensor(name, list(shape), dt, kind="ExternalInput").ap()

    def dout(name, shape):
        return nc.dram_tensor(name, list(shape), F32, kind="ExternalOutput").ap()

    xp = din("xp", [T, D])
    xs = din("xs", [NS, D])
    spool = din("spool", [NS * 15, 512])
    sgla = din("sgla", [NS, 4, 64, 128])
    ck = din("ck", [NS, 256, 1024])
    cv = din("cv", [NS, 256, 1024])
    memp = din("memp", [256, D])
    w_in = din("w_in", [D, 2064])
    w_fu = din("w_fu", [32, 256])
    w_pool = din("w_pool", [128, 4, 128])
    pscale = din("pscale", [128, 4])
    gla_g = din("gla_g", [128, 4])
    w_out = din("w_out", [D, D])
    w_km = din("w_km", [D, D])
    w_vm = din("w_vm", [D, D])
    w_qm = din("w_qm", [D, D])
    w_om = din("w_om", [D, D])
    w_up = din("w_up", [D, 4096])
    w_down = din("w_down", [4096, D])
    gains = din("gains", [128, 5, 8])
    c_ident = din("c_ident", [128, 128])
    c_tri = din("c_tri", [128, 128])
    c_trirev = din("c_trirev", [128, 128])
    c_mask4 = din("c_mask4", [128, 512])
    c_e = din("c_e", [16, 3, 16, 128])
    c_sel4 = din("c_sel4", [4, 8])
    c_invcnt = din("c_invcnt", [128, 4, 16])

    y_p = dout("y_p", [T, D])
    y_s = dout("y_s", [NS, D])
    pool_p = dout("pool_p", [15, 512])
    gla_p = dout("gla_p", [4, 64, 128])
    mk_p = dout("mk_p", [256, 1024])
    mv_p = dout("mv_p", [256, 1024])
    pool_s = dout("pool_s", [NS, 15, 512])
    gla_s = dout("gla_s", [NS, 4, 64, 128])

    S = Sched(nc)

    def cut(k):
        if CUT == k:
            S.dead = True

    def MM(out, lhsT, rhs, start, stop, reads, wb):
        S.add('pe', lambda e: e.matmul(out, lhsT=lhsT, rhs=rhs, start=start, stop=stop), reads=reads, writes=[wb])

    def TR(out, in_, ident, reads, wb):
        S.add('pe', lambda e: e.transpose(out, in_, ident), reads=reads, writes=[wb])

    def ACT(out, in_, func, reads, writes, **kw):
        S.add('act', lambda e: e.activation(out=out, in_=in_, func=func, **kw), reads=reads, writes=writes)

    def DMA(q, out, in_, reads, writes, is_output=False):
        S.add(q, lambda e: e.dma_start(out=out, in_=in_), reads=reads, writes=writes, dma=True, is_output=is_output)

    def TT_(eng, out, in0, in1, op, reads, writes):
        S.add(eng, lambda e: e.tensor_tensor(out=out, in0=in0, in1=in1, op=op), reads=reads, writes=writes)

    def TS(eng, out, in0, s1, op0, reads, writes, s2=None, op1=None):
        if op1 is None:
            S.add(eng, lambda e: e.tensor_scalar(out=out, in0=in0, scalar1=s1, scalar2=None, op0=op0), reads=reads, writes=writes)
        else:
            S.add(eng, lambda e: e.tensor_scalar(out=out, in0=in0, scalar1=s1, scalar2=s2, op0=op0, op1=op1), reads=reads, writes=writes)

    def STT(out, in0, scalar, in1, op0, op1, reads, writes):
        S.add('dve', lambda e: e.scalar_tensor_tensor(out=out, in0=in0, scalar=scalar, in1=in1, op0=op0, op1=op1), reads=reads, writes=writes)

    def CP(eng, out, in_, reads, writes):
        if eng == 'act':
            S.add('act', lambda e: e.copy(out=out, in_=in_), reads=reads, writes=writes)
        else:
            S.add(eng, lambda e: e.tensor_copy(out=out, in_=in_), reads=reads, writes=writes)

    with ExitStack() as top:
        sem_e = {e: top.enter_context(nc.semaphore(f"s_{e}")) for e in ('pe', 'act', 'dve', 'pool')}
        sem_d = {q: [top.enter_context(nc.semaphore(f"d_{q}{i}")) for i in range(10)] for q in ('sp', 'act', 'pool')}
        PS = [top.enter_context(nc.psum_tensor(f"ps{i}", [128, 512], F32)) for i in range(8)]
        B_PS = [Buf(f"ps{i}", excl=True) for i in range(8)]
        ps_rr = [0]

        PSR = [PS[6], PS[7]]
        B_PSR = [B_PS[6], B_PS[7]]

        ps_lo, ps_n = [0], [6]

        def nps():
            i = ps_lo[0] + ps_rr[0] % ps_n[0]
            ps_rr[0] += 1
            return PS[i], B_PS[i]

        def sb(es, name, shape, dt):
            return es.enter_context(nc.sbuf_tensor(name, list(shape), dt))

        xT = sb(top, "xT", [128, 8, TT], F32)
        B_xT = [[Buf(f"xT{c}_{b}") for b in range(9)] for c in range(8)]

        def xbufs(t0, n, chunks=range(8)):
            out = []
            for c in chunks:
                for b in range(t0 // 256, (t0 + n - 1) // 256 + 1):
                    out.append(B_xT[c][b])
            return out

        mkT = sb(top, "mkT", [128, 8, 256], BF16)
        mvb = sb(top, "mvb", [128, 2, 1024], BF16)
        B_mkT, B_mvb = Buf("mkT"), Buf("mvb")
        ident = sb(top, "ident", [128, 128], F32)
        ones_bf = sb(top, "ones_bf", [128, 128], BF16)
        gn = sb(top, "gn", [128, 5, 8], F32)
        B_c = Buf("consts")
        DMA('sp', ident[:], c_ident, [], [B_c])
        DMA('sp', gn[:], gains, [], [B_c])
        S.add('dve', lambda e: e.memset(ones_bf[:], 1.0), writes=[B_c])

        stage_rr = [0]

        def load_w(es_bufs, dst3, src3, B_dst, ncols, nch):
            stg, B_stg = es_bufs
            per = max(1, 2048 // ncols)
            c = 0
            k = 0
            while c < nch:
                m = min(per, nch - c)
                i = stage_rr[0] % len(stg)
                stage_rr[0] += 1
                sv = stg[i][:, 0:m * ncols].rearrange("p (a b) -> p a b", a=m)
                DMA('sp', sv, src3[:, c:c + m, :], [], [B_stg[i]])
                eng = ('act', 'dve')[k % 2]
                Bd = B_dst[k] if isinstance(B_dst, list) else B_dst
                CP(eng, dst3[:, c:c + m, :], sv, [B_stg[i]], [Bd])
                c += m
                k += 1

        def rmsnorm(tmp, src3, B_src, gidx, dst3, B_dst, n, nfeat=1024.0):
            sq, B_sq, rstd, B_rstd = tmp
            ps, B_ps = nps()
            for c in range(8):
                j = c % 2
                ACT(sq[j][:, 0:n], src3[:, c, :], AF.Square, B_src, [B_sq[j]])
                MM(ps[:, 0:n], ones_bf[:], sq[j][:, 0:n], c == 0, c == 7, [B_sq[j], B_c], B_ps)
            ACT(rstd[:, 0:n], ps[:, 0:n], AF.Ln, [B_ps], [B_rstd], scale=1.0 / nfeat, bias=EPS)
            ACT(rstd[:, 0:n], rstd[:, 0:n], AF.Exp, [B_rstd], [B_rstd], scale=-0.5)
            for c in range(8):
                STT(dst3[:, c, :], src3[:, c, :], gn[:, gidx, c:c + 1], rstd[:, 0:n], ALU.mult, ALU.mult,
                    list(B_src) + [B_rstd, B_c], (list(B_dst) if isinstance(B_dst, (list, tuple)) else [B_dst]))

        def proj_fm(wbf, B_w, col0, hT3, B_h, n, nk=8, mcols=128):
            ps, B_ps = nps()
            for k in range(nk):
                MM(ps[0:mcols, 0:n], wbf[:, k, col0:col0 + mcols], hT3[:, k, :], k == 0, k == nk - 1,
                   (list(B_w) if isinstance(B_w, (list, tuple)) else [B_w])
                   + (list(B_h) if isinstance(B_h, (list, tuple)) else [B_h]), B_ps)
            return ps, B_ps

        def proj_tm(wbf, B_w, col0, ncols, hT3, B_h, tok0, ntok, nk=8):
            ps, B_ps = nps()
            for k in range(nk):
                MM(ps[0:ntok, 0:ncols], hT3[:, k, tok0:tok0 + ntok], wbf[:, k, col0:col0 + ncols], k == 0, k == nk - 1,
                   (list(B_w) if isinstance(B_w, (list, tuple)) else [B_w]) + [B_h], B_ps)
            return ps, B_ps

        def add_into_x(ps, B_ps, ct, t0, n):
            xv = xT[:, ct, t0:t0 + n]
            bl = xbufs(t0, n, [ct])
            TT_('dve', xv, ps[:, 0:n], xv, ALU.add, [B_ps] + bl, bl)

        with ExitStack() as es:
            stg = [sb(es, f"stg{i}", [128, 2048], F32) for i in range(4)]
            B_stg = [Buf(f"stg{i}") for i in range(4)]
            xrow = [sb(es, f"xrow{i}", [128, 1024], F32) for i in range(4)]
            B_xrow = [Buf(f"xrow{i}") for i in range(4)]
            memT = sb(es, "memT", [128, 8, 256], F32)
            B_memT = Buf("memT")
            mnT = sb(es, "mnT", [128, 8, 256], BF16)
            B_mnT = Buf("mnT")
            wkm = sb(es, "wkm", [128, 8, 1024], BF16)
            wvm = sb(es, "wvm", [128, 8, 1024], BF16)
            B_wkm, B_wvm = [Buf() for _ in range(4)], [Buf() for _ in range(4)]
            mkf = sb(es, "mkf", [128, 2, 1024], F32)
            mvf = sb(es, "mvf", [128, 2, 1024], F32)
            B_mkf, B_mvf = Buf("mkf"), Buf("mvf")
            sq = [sb(es, f"sq{i}", [128, 512], BF16) for i in range(2)]
            B_sq = [Buf(), Buf()]
            rstd = sb(es, "rstd", [128, 512], F32)
            B_rstd = Buf()
            ntmp = (sq, B_sq, rstd, B_rstd)

            def load_rows_T(src_rows, nrows, dstT, t0, bufs_dst, i):
                DMA('sp', xrow[i][0:nrows, :], src_rows, [], [B_xrow[i]])
                for half in range(2):
                    ps, B_ps = nps()
                    for c4 in range(4):
                        c = half * 4 + c4
                        TR(ps[:, c4 * 128:c4 * 128 + nrows], xrow[i][0:nrows, c * 128:(c + 1) * 128], ident[0:nrows, 0:nrows],
                           [B_xrow[i], B_c], B_ps)
                    eng = ('act', 'dve')[half]
                    CP(eng, dstT[:, half * 4:half * 4 + 4, t0:t0 + nrows],
                       ps[:, :].rearrange("p (a b) -> p a b", a=4)[:, :, 0:nrows], [B_ps], bufs_dst(half))

            for tt in range(16):
                load_rows_T(xp[tt * 128:(tt + 1) * 128, :], 128, xT, tt * 128,
                            lambda half, tt=tt: xbufs(tt * 128, 128, range(half * 4, half * 4 + 4)), tt % 4)
            load_rows_T(xs[:, :], NS, xT, T, lambda half: xbufs(T, NS, range(half * 4, half * 4 + 4)), 0)
            for mt in range(2):
                load_rows_T(memp[mt * 128:(mt + 1) * 128, :], 128, memT, mt * 128, lambda half: [B_memT], (mt + 1) % 4)
            if CUT >= 2:
                rmsnorm(ntmp, memT, [B_memT], 1, mnT, B_mnT, 256)
            wkm_v = w_km.rearrange("(c p) f -> p c f", p=128)
            wvm_v = w_vm.rearrange("(c p) f -> p c f", p=128)
            if CUT >= 3:
                load_w((stg, B_stg), wkm, wkm_v, B_wkm, 1024, 8)
                load_w((stg, B_stg), wvm, wvm_v, B_wvm, 1024, 8)
            for mt in range(2 if CUT >= 4 else 0):
                for half in range(2):
                    ps, B_ps = proj_tm(wkm, B_wkm, half * 512, 512, mnT, B_mnT, mt * 128, 128)
                    CP('act', mkf[:, mt, half * 512:(half + 1) * 512], ps[:, :], [B_ps], [B_mkf])
                    ps, B_ps = proj_tm(wvm, B_wvm, half * 512, 512, mnT, B_mnT, mt * 128, 128)
                    CP('act', mvf[:, mt, half * 512:(half + 1) * 512], ps[:, :], [B_ps], [B_mvf])
                    CP('dve', mvb[:, mt, half * 512:(half + 1) * 512], ps[:, :], [B_ps], [B_mvb])
            DMA('sp', mk_p.rearrange("(a p) f -> p a f", p=128), mkf[:], [B_mkf], [], is_output=True)
            DMA('sp', mv_p.rearrange("(a p) f -> p a f", p=128), mvf[:], [B_mvf], [], is_output=True)
            for c in range(8 if CUT >= 5 else 0):
                ps, B_ps = proj_fm(wkm, B_wkm, c * 128, mnT, B_mnT, 256)
                CP(('act', 'dve')[c % 2], mkT[:, c, :], ps[:, 0:256], [B_ps], [B_mkT])
            S.barrier()

        BLK = 256
        cut(9)
        with ExitStack() as es1:
            win = sb(es1, "win", [128, 8, 2064], BF16)
            wout = sb(es1, "wout", [128, 8, 1024], BF16)
            wpool = sb(es1, "wpool", [128, 4, 128], BF16)
            wfu = sb(es1, "wfu", [32, 256], BF16)
            psc = sb(es1, "psc", [128, 4], F32)
            glag = sb(es1, "glag", [128, 4], F32)
            tri = sb(es1, "tri", [128, 128], F32)
            trirev = sb(es1, "trirev", [128, 128], F32)
            mask4 = sb(es1, "mask4", [128, 512], F32)
            invc = sb(es1, "invc", [128, 4, 16], F32)
            onesLH = sb(es1, "onesLH", [16, 2, 128], BF16)
            B_win_a, B_win_b, B_wout, B_k1 = [Buf() for _ in range(8)], [Buf() for _ in range(8)], [Buf() for _ in range(4)], Buf("k1")
            B_win = B_win_a + B_win_b
            hT = sb(es1, "hT", [128, 8, BLK], BF16)
            sq = [sb(es1, f"sq1_{i}", [128, 512], BF16) for i in range(2)]
            rstd = sb(es1, "rstd1", [128, 512], F32)
            B_hT, B_sq, B_rstd = Buf("hT"), [Buf(), Buf()], Buf()
            ntmp = (sq, B_sq, rstd, B_rstd)
            flT = sb(es1, "flT", [32, BLK], BF16)
            ez = sb(es1, "ez", [128, 256], F32)
            sp = sb(es1, "sp", [128, 2, 256], F32)
            sg = sb(es1, "sg", [128, 4, BLK], F32)
            uT = sb(es1, "uT", [128, 4, 16 + BLK], F32)
            tA = sb(es1, "tA", [128, 16 + BLK], F32)
            tB = sb(es1, "tB", [128, 16 + BLK], F32)
            pooled = sb(es1, "pooled", [128, 4, BLK], BF16)
            of32 = sb(es1, "of32", [128, 4, BLK], F32)
            osq = sb(es1, "osq", [128, 4, BLK], BF16)
            rst2 = sb(es1, "rst2", [128, BLK], F32)
            t1 = sb(es1, "t1", [128, BLK], F32)
            mixedT = sb(es1, "mixedT", [128, 8, BLK], BF16)
            B_fl, B_ez, B_sp, B_sg, B_uT, B_tA, B_tB = Buf(), Buf(), Buf(), Buf(), Buf(), Buf(), Buf()
            B_pooled, B_of, B_osq, B_rst2, B_t1, B_mixed = Buf(), Buf(), Buf(), Buf(), Buf(), Buf()

            with ExitStack() as esl:
                stg = [sb(esl, f"stg1_{i}", [128, 2048], F32) for i in range(4)]
                B_stg = [Buf() for _ in range(4)]
                ctmp = sb(esl, "ctmp", [128, 768], F32)
                B_ctmp = Buf()
                win_v = w_in.rearrange("(c p) f -> p c f", p=128)
                load_w((stg, B_stg), win[:, :, 0:1032], win_v[:, :, 0:1032], B_win_a, 1032, 8)
                load_w((stg, B_stg), win[:, :, 1032:2064], win_v[:, :, 1032:2064], B_win_b, 1032, 8)
                load_w((stg, B_stg), wout, w_out.rearrange("(c p) f -> p c f", p=128), B_wout, 1024, 8)
                DMA('sp', ctmp[:, 0:512], w_pool.rearrange("p g d -> p (g d)"), [], [B_ctmp])
                DMA('sp', ctmp[0:32, 512:768], w_fu, [], [B_ctmp])
                CP('dve', wpool[:].rearrange("p g d -> p (g d)"), ctmp[:, 0:512], [B_ctmp], [B_k1])
                CP('dve', wfu[:], ctmp[0:32, 512:768], [B_ctmp], [B_k1])
                for (dst, src) in ((psc, pscale), (glag, gla_g), (tri, c_tri), (trirev, c_trirev), (mask4, c_mask4),
                                   (invc, c_invcnt)):
                    DMA('sp', dst[:], src, [], [B_k1])
                S.add('dve', lambda e: e.memset(onesLH[:], 0.0), writes=[B_k1])
                S.add('dve', lambda e: e.memset(onesLH[:, 0, 0:64], 1.0), writes=[B_k1])
                S.add('dve', lambda e: e.memset(onesLH[:, 1, 64:128], 1.0), writes=[B_k1])
                S.add('dve', lambda e: e.memset(flT[:], 1.0), writes=[B_fl])
                S.add('dve', lambda e: e.memset(uT[:], 0.0), writes=[B_uT])
                S.barrier()
            cut(10)

            def s1_common(t0, n):
                h3 = hT[:, :, 0:n]
                rmsnorm(ntmp, xT[:, :, t0:t0 + n], xbufs(t0, n), 0, h3, B_hT, n)
                ps, Bp = proj_fm(win, B_win, 2048, h3, B_hT, n, mcols=16)
                CP('act', flT[0:16, 0:n], ps[0:16, 0:n], [Bp], [B_fl])
                ntt = max(1, n // 128)
                tw = min(n, 128)
                for tt in range(ntt):
                    ps, Bp = nps()
                    MM(ps[0:tw, 0:256], flT[0:32, tt * 128:tt * 128 + tw], wfu[0:32, :], True, True, [B_fl, B_k1], Bp)
                    ACT(ez[0:tw, :], ps[0:tw, 0:256], AF.Exp, [Bp], [B_ez], scale=-1.0)
                    ACT(sp[0:tw, tt, :], ez[0:tw, :], AF.Ln, [B_ez], [B_sp], bias=1.0)
                for hh in range(4):
                    ps, Bp = proj_fm(win, B_win, 1536 + hh * 128, h3, B_hT, n)
                    ACT(sg[:, hh, 0:n], ps[:, 0:n], AF.Silu, [Bp], [B_sg])
                for g in range(4):
                    ps, Bp = proj_fm(win, B_win, g * 128, h3, B_hT, n)
                    CP('act', uT[:, g, 16:16 + n], ps[:, 0:n], [Bp], [B_uT])
                return h3

            def s1_tail(t0, n):
                for g in range(4):
                    ps, Bp = nps()
                    MM(ps[:, 0:n], wpool[:, g, :], pooled[:, g, 0:n], True, True, [B_k1, B_pooled], Bp)
                    TS('dve', mixedT[:, g, 0:n], ps[:, 0:n], psc[:, g:g + 1], ALU.mult, [Bp, B_k1], [B_mixed])
                ACT(osq[:, :, 0:n], of32[:, :, 0:n], AF.Square, [B_of], [B_osq])
                for h in range(4):
                    ps, Bp = nps()
                    MM(ps[:, 0:n], ones_bf[:], osq[:, h, 0:n], True, True, [B_osq, B_c], Bp)
                    ACT(rst2[:, 0:n], ps[:, 0:n], AF.Ln, [Bp], [B_rst2], scale=1.0 / 128.0, bias=EPS)
                    ACT(rst2[:, 0:n], rst2[:, 0:n], AF.Exp, [B_rst2], [B_rst2], scale=-0.5)
                    STT(t1[:, 0:n], of32[:, h, 0:n], glag[:, h:h + 1], rst2[:, 0:n], ALU.mult, ALU.mult,
                        [B_of, B_k1, B_rst2], [B_t1])
                    TT_('dve', mixedT[:, 4 + h, 0:n], t1[:, 0:n], sg[:, h, 0:n], ALU.mult, [B_t1, B_sg], [B_mixed])
                for ct in range(8):
                    ps, Bp = proj_fm(wout, B_wout, ct * 128, mixedT[:, :, 0:n], B_mixed, n)
                    add_into_x(ps, Bp, ct, t0, n)

            with ExitStack() as esa:
                S0 = sb(esa, "S0", [128, NS, 2, 128], F32)
                Sbf = sb(esa, "Sbf", [128, NS, 2, 128], BF16)
                stT = sb(esa, "stT", [128, 4, 240], F32)
                rowb = [sb(esa, f"rowb{i}", [128, 512], F32) for i in range(2)]
                decT = sb(esa, "decT", [128, 2, NS], F32)
                qTs = [sb(esa, f"qTs{r}", [128, 2, NS], BF16) for r in range(2)]
                kTs = sb(esa, "kTs", [128, 2, NS], F32)
                vs_bf = sb(esa, "vs_bf", [16, 512], BF16)
                us_f = sb(esa, "us_f", [16, 512], F32)
                vm = [sb(esa, f"vm{i}", [16, 512], BF16) for i in range(2)]
                wsum = sb(esa, "wsum", [128, NS], F32)
                B_S0, B_Sbf, B_stT, B_rowb = [Buf() for _ in range(NS)], Buf(), Buf(), [Buf(), Buf()]
                B_dec, B_qTs, B_kTs, B_vs, B_us, B_vm, B_ws = Buf(), Buf(), Buf(), Buf(), Buf(), [Buf(), Buf()], Buf()
                t0, n = T, NS
                for r in range(2):
                    S.add('dve', lambda e, r=r: e.memset(qTs[r][:], 0.0), writes=[B_qTs])
                for s in range(NS):
                    DMA('sp', S0[:, s, :, :], sgla[s].rearrange("(hp hr) k v -> (hr k) hp v", hr=2), [], [B_S0[s]])
                for i, (r0, nr) in enumerate(((0, 128), (128, 112))):
                    DMA('sp', rowb[i][0:nr, :], spool[r0:r0 + nr, :], [], [B_rowb[i]])
                    ps, Bp = nps()
                    for g in range(4):
                        TR(ps[:, g * 128:g * 128 + nr], rowb[i][0:nr, g * 128:(g + 1) * 128], ident[0:nr, 0:nr],
                           [B_rowb[i], B_c], Bp)
                    CP('act', stT[:, :, r0:r0 + nr], ps[:, :].rearrange("p (g r) -> p g r", g=4)[:, :, 0:nr], [Bp], [B_stT])
                DMA('sp', pool_s[:, 0:14, :], spool.rearrange("(s r) c -> s r c", r=15)[:, 1:15, :], [], [], is_output=True)
                h3 = s1_common(t0, n)
                ps, Bp = nps()
                for p in range(2):
                    TR(ps[:, p * 16:(p + 1) * 16], sp[0:16, 0, p * 128:(p + 1) * 128], ident[0:16, 0:16], [B_sp, B_c], Bp)
                ACT(decT[:].rearrange("p a s -> p (a s)"), ps[:, 0:32], AF.Exp, [Bp], [B_dec], scale=-1.0 / 16.0)
                for p in range(2):
                    ps, Bp = proj_fm(win, B_win, 512 + p * 128, h3, B_hT, n)
                    for r in range(2):
                        ACT(qTs[r][r * 64:(r + 1) * 64, p, :], ps[r * 64:(r + 1) * 64, 0:n], AF.Copy, [Bp], [B_qTs], scale=0.125)
                    ps, Bp = proj_fm(win, B_win, 768 + p * 128, h3, B_hT, n)
                    CP('dve', kTs[:, p, :], ps[:, 0:n], [Bp], [B_kTs])
                ps, Bp = proj_tm(win, B_win, 1024, 512, h3, B_hT, 0, n)
                CP('act', vs_bf[:, :], ps[0:n, :], [Bp], [B_vs])
                ps, Bp = proj_tm(win, B_win, 0, 512, h3, B_hT, 0, n)
                CP('act', us_f[:, :], ps[0:n, :], [Bp], [B_us])
                DMA('sp', pool_s[:, 14, :], us_f[:, :], [B_us], [], is_output=True)
                for g, w in enumerate((2, 4, 8, 16)):
                    st3 = stT[:, g, :].rearrange("p (s r) -> p s r", r=15)[:, :, 16 - w:15]
                    S.add('dve', lambda e, st3=st3: e.tensor_reduce(out=wsum[:, :], in_=st3, axis=AX.X, op=ALU.add),
                          reads=[B_stT], writes=[B_ws])
                    TT_('dve', wsum[:, :], wsum[:, :], uT[:, g, 16:16 + n], ALU.add, [B_ws, B_uT], [B_ws])
                    STT(pooled[:, g, 0:n], wsum[:, :], 1.0 / w, uT[:, g, 16:16 + n], ALU.mult, ALU.subtract,
                        [B_ws, B_uT], [B_pooled])
                for hp in range(2):
                    TT_('dve', S0[:, :, hp, :], S0[:, :, hp, :],
                        decT[:, hp, :].unsqueeze(2).broadcast_to([128, NS, 128]), ALU.mult,
                        B_S0 + [B_dec], B_S0)
                for s in range(NS):
                    i = s % 2
                    ACT(vm[i][:, :], vs_bf[:, :], AF.Copy, [B_vs, B_c], [B_vm[i]], scale=ident[0:16, s:s + 1])
                    if i == 0:
                        psV, BpV = nps()
                    vm4 = vm[i][:, :].rearrange("s (hp hr v) -> s hp hr v", hp=2, hr=2)
                    ov = psV[:, i * 256:(i + 1) * 256].rearrange("p (a v) -> p a v", a=2)
                    MM(ov, onesLH[:, 0, :], vm4[:, :, 0, :], True, False, [B_k1, B_vm[i]], BpV)
                    MM(ov, onesLH[:, 1, :], vm4[:, :, 1, :], False, True, [B_k1, B_vm[i]], BpV)
                    for hp in range(2):
                        STT(S0[:, s, hp, :], psV[:, i * 256 + hp * 128:i * 256 + (hp + 1) * 128], kTs[:, hp, s:s + 1],
                            S0[:, s, hp, :], ALU.mult, ALU.add, [BpV, B_kTs, B_S0[s]], [B_S0[s]])
                    DMA('sp', gla_s[s].rearrange("(hp hr) k v -> (hr k) hp v", hr=2), S0[:, s, :, :], [B_S0[s]], [],
                        is_output=True)
                CP('act', Sbf[:, 0:8], S0[:, 0:8], B_S0, [B_Sbf])
                CP('dve', Sbf[:, 8:16], S0[:, 8:16], B_S0, [B_Sbf])
                psO, BpO = nps()
                for s in range(NS):
                    for h in range(4):
                        p, r = h // 2, h % 2
                        MM(psO[:, h * 16 + s:h * 16 + s + 1], Sbf[:, s, p, :], qTs[r][:, p, s:s + 1], True, True,
                           [B_Sbf, B_qTs], BpO)
                CP('act', of32[:, :, 0:n], psO[:, 0:64].rearrange("d (h s) -> d h s", h=4), [BpO], [B_of])
                s1_tail(t0, n)
                S.add('dve', lambda e: e.memset(uT[:], 0.0), reads=[B_uT], writes=[B_uT])
                S.barrier()
            cut(11)

            with ExitStack() as esb:
                epos = [sb(esb, f"epos{i}", [128, 2, BLK], F32) for i in range(2)]
                eneg = sb(esb, "eneg", [128, 2, BLK], F32)
                eend = sb(esb, "eend", [128, 2, 256], F32)
                qdT = [sb(esb, f"qdT{i}", [128, 2, BLK], BF16) for i in range(2)]
                kiT = [[sb(esb, f"kiT{i}_{r}", [128, 2, BLK], BF16) for r in range(2)] for i in range(2)]
                kend = [sb(esb, f"kend{i}", [128, 2, 256], BF16) for i in range(2)]
                vbf = [sb(esb, f"vbf{i}", [128, 2, 512], BF16) for i in range(2)]
                sg2 = [sg, sb(esb, "sg_b", [128, 4, BLK], F32)]
                pooled2 = [pooled, sb(esb, "pooled_b", [128, 4, BLK], BF16)]
                scT = sb(esb, "scT", [128, 512], BF16)
                Sf = sb(esb, "Sf", [128, 2, 128], F32)
                Sb = [sb(esb, f"Sb{r}", [128, 2, 128], BF16) for r in range(2)]
                utm = sb(esb, "utm", [128, 512], F32)
                tm16 = sb(esb, "tm16", [128, 16], F32)
                B_epos, B_eneg, B_eend = [Buf(), Buf()], Buf(), Buf()
                B_qd, B_ki, B_kend, B_v = [Buf(), Buf()], [Buf(), Buf()], [Buf(), Buf()], [Buf(), Buf()]
                B_sg2, B_pooled2 = [B_sg, Buf()], [B_pooled, Buf()]
                B_sc, B_Sf, B_Sb, B_utm, B_tm16 = Buf(), Buf(), Buf(), Buf(), Buf()
                for i in range(2):
                    for r in range(2):
                        S.add('dve', lambda e, i=i, r=r: e.memset(kiT[i][r][:], 0.0), writes=[B_ki[i]])
                for r in range(2):
                    S.add('dve', lambda e, r=r: e.memset(Sb[r][:], 0.0), writes=[B_Sb])
                NBLK = T // BLK
                ntt = BLK // 128
                n = BLK
                h3 = hT[:, :, 0:n]

                def front_steps(bi):
                    t0 = bi * BLK
                    pb = bi % 2
                    st = {}

                    def f_norm():
                        rmsnorm(ntmp, xT[:, :, t0:t0 + n], xbufs(t0, n), 0, h3, B_hT, n)

                    def f_fl():
                        ps, Bp = proj_fm(win, B_win, 2048, h3, B_hT, n, mcols=16)
                        CP('act', flT[0:16, 0:n], ps[0:16, 0:n], [Bp], [B_fl])
                        for tt in range(ntt):
                            ps, Bp = nps()
                            MM(ps[:, 0:256], flT[0:32, tt * 128:(tt + 1) * 128], wfu[0:32, :], True, True, [B_fl, B_k1], Bp)
                            ACT(ez[:, :], ps[:, 0:256], AF.Exp, [Bp], [B_ez], scale=-1.0)
                            ACT(sp[:, tt, :], ez[:, :], AF.Ln, [B_ez], [B_sp], bias=1.0)

                    def f_gate(hh):
                        ps, Bp = proj_fm(win, B_win, 1536 + hh * 128, h3, B_hT, n)
                        ACT(sg2[pb][:, hh, 0:n], ps[:, 0:n], AF.Silu, [Bp], [B_sg2[pb]])

                    def f_u(g):
                        ps, Bp = proj_fm(win, B_win, g * 128, h3, B_hT, n)
                        CP('act', uT[:, g, 16:16 + n], ps[:, 0:n], [Bp], [B_uT])

                    def f_cr():
                        for p in range(2):
                            psC, BpC = nps()
                            for tt in range(ntt):
                                MM(psC[:, tt * 128:(tt + 1) * 128], sp[:, tt, p * 128:(p + 1) * 128], tri[:], True, True,
                                   [B_sp, B_k1], BpC)
                            ACT(epos[pb][:, p, 0:n], psC[:, 0:n], AF.Exp, [BpC], [B_epos[pb]], scale=-1.0 / 16.0)
                            ACT(eneg[:, p, 0:n], psC[:, 0:n], AF.Exp, [BpC], [B_eneg], scale=1.0 / 16.0)
                        psR, BpR = nps()
                        for tt in range(ntt):
                            MM(psR[:, tt * 256:(tt + 1) * 256], trirev[:], sp[:, tt, :], True, True, [B_sp, B_k1], BpR)
                        ACT(eend[:].rearrange("p a b -> p (a b)"), psR[:, 0:512], AF.Exp, [BpR], [B_eend], scale=-1.0 / 16.0)

                    def f_qk(p):
                        ps, Bp = proj_fm(win, B_win, 512 + p * 128, h3, B_hT, n)
                        STT(qdT[pb][:, p, 0:n], ps[:, 0:n], 0.125, epos[pb][:, p, 0:n], ALU.mult, ALU.mult,
                            [Bp, B_epos[pb]], [B_qd[pb]])
                        ps, Bp = proj_fm(win, B_win, 768 + p * 128, h3, B_hT, n)
                        for r in range(2):
                            TT_('dve', kiT[pb][r][r * 64:(r + 1) * 64, p, 0:n], ps[r * 64:(r + 1) * 64, 0:n],
                                eneg[r * 64:(r + 1) * 64, p, 0:n], ALU.mult, [Bp, B_eneg], [B_ki[pb]])

                    def f_kv(tt):
                        ps, Bp = proj_tm(win, B_win, 1024, 512, h3, B_hT, tt * 128, 128)
                        CP('act', vbf[pb][:, tt, :], ps[:, :], [Bp], [B_v[pb]])
                        ps, Bp = proj_tm(win, B_win, 768, 256, h3, B_hT, tt * 128, 128)
                        TT_('dve', kend[pb][:, tt, :], ps[:, 0:256], eend[:, tt, :], ALU.mult, [Bp, B_eend], [B_kend[pb]])

                    def f_utm():
                        ps, Bp = proj_tm(win, B_win, 0, 512, h3, B_hT, (ntt - 1) * 128, 128)
                        CP('act', utm[:, :], ps[:, :], [Bp], [B_utm])
                        DMA('sp', pool_p, utm[113:128, :], [B_utm], [], is_output=True)

                    def f_pool():
                        W = 16 + n
                        for g, w in enumerate((2, 4, 8, 16)):
                            a = uT[:, g, 0:W]
                            src, Bsrc = a, B_uT
                            sh = 1
                            k = 0
                            while sh < w:
                                dst, Bdst = ((tA, B_tA), (tB, B_tB))[k % 2]
                                lo = 2 * sh - 1
                                TT_('pool', dst[:, lo:W], src[:, lo:W], src[:, lo - sh:W - sh], ALU.add, [Bsrc], [Bdst])
                                src, Bsrc = dst, Bdst
                                sh *= 2
                                k += 1
                            STT(pooled2[pb][:, g, 0:n], src[:, 16:W], 1.0 / w, a[:, 16:W], ALU.mult, ALU.subtract,
                                [Bsrc, B_uT], [B_pooled2[pb]])
                            if bi == 0:
                                TT_('dve', tm16[:, :], src[:, 16:32], invc[:, g, :], ALU.mult, [Bsrc, B_k1], [B_tm16])
                                TT_('dve', pooled2[pb][:, g, 0:16], tm16[:, :], a[:, 16:32], ALU.subtract,
                                    [B_tm16, B_uT], [B_pooled2[pb]])
                        CP('pool', uT[:, :, 0:16], uT[:, :, n:n + 16], [B_uT, B_tA, B_tB], [B_uT])

                    lst = [f_norm, f_fl, lambda: f_gate(0), lambda: f_gate(1), f_cr, lambda: f_gate(2), lambda: f_gate(3),
                           lambda: f_u(0), lambda: f_u(1), lambda: f_qk(0), lambda: f_u(2), lambda: f_qk(1), lambda: f_u(3),
                           lambda: f_kv(0), lambda: f_kv(1)]
                    if bi == NBLK - 1:
                        lst.append(f_utm)
                    lst.append(f_pool)
                    return lst

                def gla_steps(bi):
                    pb = bi % 2
                    lst = []
                    for tt in range(ntt):
                        tok = slice(tt * 128, (tt + 1) * 128)
                        first = (bi == 0 and tt == 0)
                        hold = {}

                        def g_scores(tok=tok):
                            psS, BpS = nps()
                            for h in range(4):
                                p, r = h // 2, h % 2
                                MM(psS[:, h * 128:(h + 1) * 128], kiT[pb][r][:, p, tok], qdT[pb][:, p, tok], True, True,
                                   [B_ki[pb], B_qd[pb]], BpS)
                            TT_('dve', scT[:, :], psS[:, :], mask4[:, :], ALU.mult, [BpS, B_k1], [B_sc])

                        def g_o(tok=tok, tt=tt, first=first):
                            psO, BpO = nps()
                            for h in range(4):
                                p, r = h // 2, h % 2
                                MM(psO[:, h * 128:(h + 1) * 128], vbf[pb][:, tt, h * 128:(h + 1) * 128],
                                   scT[:, h * 128:(h + 1) * 128], True, first, [B_v[pb], B_sc], BpO)
                                if not first:
                                    MM(psO[:, h * 128:(h + 1) * 128], Sb[r][:, p, :], qdT[pb][:, p, tok], False, True,
                                       [B_Sb, B_qd[pb]], BpO)
                            CP('act', of32[:, :, tok], psO[:, :].rearrange("d (h t) -> d h t", h=4), [BpO], [B_of])

                        def g_state(tt=tt, first=first):
                            psU, BpU = nps()
                            for p in range(2):
                                MM(psU[:, p * 256:(p + 1) * 256], kend[pb][:, tt, p * 128:(p + 1) * 128],
                                   vbf[pb][:, tt, p * 256:(p + 1) * 256], True, True, [B_kend[pb], B_v[pb]], BpU)
                            for h in range(4):
                                p, r = h // 2, h % 2
                                P0, P1 = r * 64, (r + 1) * 64
                                uu = psU[P0:P1, p * 256 + r * 128:p * 256 + (r + 1) * 128]
                                if first:
                                    CP('dve', Sf[P0:P1, p, :], uu, [BpU], [B_Sf])
                                else:
                                    STT(Sf[P0:P1, p, :], Sf[P0:P1, p, :], epos[pb][P0:P1, p, tt * 128 + 127:tt * 128 + 128], uu,
                                        ALU.mult, ALU.add, [B_Sf, B_epos[pb], BpU], [B_Sf])
                            for r in range(2):
                                CP('act', Sb[r][r * 64:(r + 1) * 64, :, :], Sf[r * 64:(r + 1) * 64, :, :], [B_Sf], [B_Sb])

                        lst += [g_scores, g_o, g_state]
                    return lst

                def tail_steps(bi):
                    t0 = bi * BLK
                    pb = bi % 2

                    def t_pool():
                        for g in range(4):
                            ps, Bp = nps()
                            MM(ps[:, 0:n], wpool[:, g, :], pooled2[pb][:, g, 0:n], True, True, [B_k1, B_pooled2[pb]], Bp)
                            TS('dve', mixedT[:, g, 0:n], ps[:, 0:n], psc[:, g:g + 1], ALU.mult, [Bp, B_k1], [B_mixed])
                        ACT(osq[:, :, 0:n], of32[:, :, 0:n], AF.Square, [B_of], [B_osq])

                    def t_epi(h):
                        ps, Bp = nps()
                        MM(ps[:, 0:n], ones_bf[:], osq[:, h, 0:n], True, True, [B_osq, B_c], Bp)
                        ACT(rst2[:, 0:n], ps[:, 0:n], AF.Ln, [Bp], [B_rst2], scale=1.0 / 128.0, bias=EPS)
                        ACT(rst2[:, 0:n], rst2[:, 0:n], AF.Exp, [B_rst2], [B_rst2], scale=-0.5)
                        STT(t1[:, 0:n], of32[:, h, 0:n], glag[:, h:h + 1], rst2[:, 0:n], ALU.mult, ALU.mult,
                            [B_of, B_k1, B_rst2], [B_t1])
                        TT_('dve', mixedT[:, 4 + h, 0:n], t1[:, 0:n], sg2[pb][:, h, 0:n], ALU.mult, [B_t1, B_sg2[pb]], [B_mixed])

                    def t_wout(ct):
                        ps, Bp = proj_fm(wout, B_wout, ct * 128, mixedT[:, :, 0:n], B_mixed, n)
                        add_into_x(ps, Bp, ct, t0, n)

                    return [t_pool] + [lambda h=h: t_epi(h) for h in range(4)] + [lambda ct=ct: t_wout(ct) for ct in range(8)]

                ps_lo[0], ps_n[0] = 0, 4
                for f in front_steps(0):
                    f()
                for bi in range(NBLK):
                    Ls = gla_steps(bi) + tail_steps(bi)
                    Ds = front_steps(bi + 1) if bi + 1 < NBLK else []
                    i = j = 0
                    while i < len(Ls) or j < len(Ds):
                        if j < len(Ds):
                            ps_lo[0], ps_n[0] = 0, 4
                            Ds[j]()
                            j += 1
                        if i < len(Ls):
                            ps_lo[0], ps_n[0] = 4, 4
                            Ls[i]()
                            i += 1
                ps_lo[0], ps_n[0] = 0, 6
                DMA('sp', gla_p.rearrange("(hp hr) k v -> (hr k) hp v", hr=2), Sf[:, :, :], [B_Sf], [], is_output=True)
                S.barrier()
        cut(13)
        with ExitStack() as es2:
            wqm = sb(es2, "wqm", [128, 8, 1024], BF16)
            wom = sb(es2, "wom", [128, 8, 1024], BF16)
            sel4 = sb(es2, "sel4", [4, 8], BF16)
            ohs = sb(es2, "ohs", [16, 16, 128], BF16)
            B_wom, B_k2 = [Buf() for _ in range(4)], Buf("k2")
            sq = [sb(es2, f"sq2_{i}", [128, 512], BF16) for i in range(2)]
            rstd = sb(es2, "rstd2", [128, 512], F32)
            B_sq, B_rstd = [Buf(), Buf()], Buf()
            ntmp = (sq, B_sq, rstd, B_rstd)
            stg = [sb(es2, f"stg2_{i}", [128, 2048], F32) for i in range(2)]
            B_stg = [Buf(), Buf()]
            B_wqm = [Buf(f"wqm{g}") for g in range(4)]
            DMA('sp', stg[0][0:4, 0:8], c_sel4, [], [B_stg[0]])
            CP('dve', sel4[:, :], stg[0][0:4, 0:8], [B_stg[0]], [B_k2])
            DMA('sp', stg[1][0:16, :], c_e[:, 0].rearrange("a s m -> a (s m)"), [], [B_stg[1]])
            CP('dve', ohs[:].rearrange("a s m -> a (s m)"), stg[1][0:16, :], [B_stg[1]], [B_k2])
            wqm_v = w_qm.rearrange("(c p) f -> p c f", p=128)
            stage_rr[0] = 0
            for g in range(4):
                load_w((stg, B_stg), wqm[:, :, g * 256:(g + 1) * 256], wqm_v[:, :, g * 256:(g + 1) * 256], B_wqm[g], 256, 8)
            load_w((stg, B_stg), wom, w_om.rearrange("(c p) f -> p c f", p=128), B_wom, 1024, 8)
            cut(20)

            def s2_tail(t0, n, ctx3, B_ctx):
                for ct in range(8):
                    ps, Bp = proj_fm(wom, B_wom, ct * 128, ctx3, B_ctx, n)
                    add_into_x(ps, Bp, ct, t0, n)

            with ExitStack() as esa:
                hms = sb(esa, "hms", [128, 8, NS], BF16)
                qs_bf = sb(esa, "qs_bf", [16, 1024], BF16)
                qm = [sb(esa, f"qm{i}", [16, 1024], BF16) for i in range(2)]
                qb = sb(esa, "qb", [128, 1024], F32)
                Kt = [sb(esa, f"Kt{i}", [128, 2, 1024], F32) for i in range(2)]
                Vb = [sb(esa, f"Vb{i}", [128, 2, 1024], BF16) for i in range(2)]
                junk = sb(esa, "junk", [128, 256], F32)
                SC = sb(esa, "SC", [128, NS, 2, 4], F32)
                Ebf = sb(esa, "Ebf", [128, NS, 2, 4], BF16)
                rs = sb(esa, "rs", [4, NS], F32)
                ctxs = [sb(esa, f"ctxs{i}", [4, 1024], BF16) for i in range(2)]
                ctxTs = sb(esa, "ctxTs", [128, 8, NS], BF16)
                B_hms, B_qs, B_qm, B_qb, B_Kt, B_Vt, B_Vb = Buf(), Buf(), [Buf(), Buf()], Buf(), [Buf(), Buf()], [Buf(), Buf()], [Buf(), Buf()]
                B_junk, B_SC, B_E, B_rs, B_ctxs, B_ctxTs = Buf(), [Buf() for _ in range(NS)], [Buf() for _ in range(NS)], [Buf() for _ in range(NS)], [Buf(), Buf()], Buf()
                NB = 512
                hm = sb(esa, "hm", [128, 8, NB], BF16)
                qmT = sb(esa, "qmT", [128, 8, NB], BF16)
                ex = [sb(esa, f"ex{i}", [128, 2, NB], BF16) for i in range(2)]
                rinv = sb(esa, "rinv", [128, NB], F32)
                ctxT = sb(esa, "ctxT", [128, 8, NB], BF16)
                B_hm, B_qmT, B_ex, B_rinv, B_ctxT = Buf(), Buf(), [Buf(), Buf()], Buf(), Buf()
                psM, BpM = PSR[0], B_PSR[0]
                psT, BpT = PSR[1], B_PSR[1]

                rmsnorm(ntmp, xT[:, :, T:T + NS], xbufs(T, NS), 2, hms[:, :, :], B_hms, NS)
                for half in range(2):
                    ps, Bp = proj_tm(wqm, B_wqm[2 * half:2 * half + 2], half * 512, 512, hms, B_hms, 0, NS)
                    CP('act', qs_bf[:, half * 512:(half + 1) * 512], ps[0:NS, :], [Bp], [B_qs])

                def s_dma(s):
                    i = s % 2
                    DMA('sp', Kt[i][:, :, :], ck[s].rearrange("(mt p) f -> p mt f", p=128), [], [B_Kt[i]])

                def s_dma_v(s):
                    i = s % 2
                    DMA('pool', Vb[i][:, :, :], cv[s].rearrange("(mt p) f -> p mt f", p=128), [], [B_Vb[i]])

                def s_qm(s):
                    i = s % 2
                    ACT(qm[i][:, :], qs_bf[:, :], AF.Copy, [B_qs, B_c], [B_qm[i]], scale=ident[0:16, s:s + 1])

                def s_bcast(s):
                    for half in range(2):
                        ps, Bp = nps()
                        MM(ps[:, :], ohs[:, s, :], qs_bf[:, half * 512:(half + 1) * 512], True, True, [B_k2, B_qs], Bp)
                        CP('act', qb[:, half * 512:(half + 1) * 512], ps[:, :], [Bp], [B_qb])

                def s_scores(s):
                    i = s % 2
                    for mt in range(2):
                        for h in range(4):
                            S.add('dve', lambda e, i=i, mt=mt, h=h, s=s: e.scalar_tensor_tensor(
                                out=junk[:, :], in0=Kt[i][:, mt, h * 256:(h + 1) * 256], scalar=1.0 / 16.0,
                                in1=qb[:, h * 256:(h + 1) * 256], op0=ALU.mult, op1=ALU.mult,
                                accum_out=SC[:, s, mt, h:h + 1]),
                                reads=[B_Kt[i], B_qb], writes=[B_junk, B_SC[s]])

                def s_vcast_exp(s):
                    i = s % 2
                    ACT(Ebf[:, s].rearrange("p a h -> p (a h)"), SC[:, s].rearrange("p a h -> p (a h)"), AF.Exp,
                        [B_SC[s]], [B_E[s]])

                def s_sums(s):
                    for mt in range(2):
                        MM(psM[0:4, s:s + 1], Ebf[:, s, mt, :], ones_bf[:, 0:1], mt == 0, mt == 1, [B_E[s], B_c], BpM)
                    S.add('dve', lambda e, s=s: e.reciprocal(out=rs[:, s:s + 1], in_=psM[0:4, s:s + 1]), reads=[BpM],
                          writes=[B_rs[s]])

                def s_ctx(s):
                    i = s % 2
                    for half in range(2):
                        ps, Bp = nps()
                        for mt in range(2):
                            MM(ps[0:4, :], Ebf[:, s, mt, :], Vb[i][:, mt, half * 512:(half + 1) * 512], mt == 0, mt == 1,
                               [B_E[s], B_Vb[i]], Bp)
                        ACT(ctxs[i][:, half * 512:(half + 1) * 512], ps[0:4, :], AF.Copy, [Bp, B_rs[s]], [B_ctxs[i]],
                            scale=rs[0:4, s:s + 1])

                def s_sel(s):
                    i = s % 2
                    for c in range(8):
                        MM(psT[:, c * 16 + s:c * 16 + s + 1], ctxs[i][:, c * 128:(c + 1) * 128], sel4[:, c:c + 1], True, True,
                           [B_ctxs[i], B_k2], BpT)

                def ok(s):
                    return 0 <= s < NS

                s_dma(0)
                s_dma_v(0)

                def p_front(bi):
                    t0 = bi * NB
                    rmsnorm(ntmp, xT[:, :, t0:t0 + NB], xbufs(t0, NB), 2, hm, B_hm, NB)

                def p_qm(bi):
                    for c in range(8):
                        ps, Bp = proj_fm(wqm, B_wqm[c // 2], c * 128, hm, B_hm, NB)
                        CP(('act', 'dve')[c % 2], qmT[:, c, :], ps[:, 0:NB], [Bp], [B_qmT])

                def p_scores(h):
                    e_, Be = ex[h % 2], B_ex[h % 2]
                    for mt in range(2):
                        ps, Bp = nps()
                        for j in range(2):
                            MM(ps[:, 0:NB], mkT[:, 2 * h + j, mt * 128:(mt + 1) * 128], qmT[:, 2 * h + j, :], j == 0, j == 1,
                               [B_mkT, B_qmT], Bp)
                        ACT(e_[:, mt, :], ps[:, 0:NB], AF.Exp, [Bp], [Be], scale=1.0 / 16.0)

                def p_ctx(h):
                    e_, Be = ex[h % 2], B_ex[h % 2]
                    ps, Bp = nps()
                    for mt in range(2):
                        MM(ps[:, 0:NB], ones_bf[:], e_[:, mt, :], mt == 0, mt == 1, [B_c, Be], Bp)
                    ACT(rinv[:, :], ps[:, 0:NB], AF.Ln, [Bp], [B_rinv])
                    ACT(rinv[:, :], rinv[:, :], AF.Exp, [B_rinv], [B_rinv], scale=-1.0)
                    for j in range(2):
                        ps, Bp = nps()
                        for mt in range(2):
                            MM(ps[:, 0:NB], mvb[:, mt, (2 * h + j) * 128:(2 * h + j + 1) * 128], e_[:, mt, :], mt == 0, mt == 1,
                               [B_mvb, Be], Bp)
                        TT_('dve', ctxT[:, 2 * h + j, :], ps[:, 0:NB], rinv[:, :], ALU.mult, [Bp, B_rinv], [B_ctxT])

                it = 0
                nblk = T // NB
                p_front(0)
                p_qm(0)
                for bi in range(nblk):
                    p_scores(0)
                    for h in range(4):
                        if ok(it + 1):
                            s_dma(it + 1)
                        if ok(it):
                            s_bcast(it)
                        if h + 1 < 4:
                            p_scores(h + 1)
                        if ok(it - 1):
                            s_sums(it - 1)
                        if ok(it - 2):
                            s_sel(it - 2)
                        p_ctx(h)
                        if ok(it - 1):
                            s_ctx(it - 1)
                        if ok(it + 1):
                            s_dma_v(it + 1)
                        if ok(it):
                            s_scores(it)
                            s_vcast_exp(it)
                        it += 1
                    if bi + 1 < nblk:
                        p_front(bi + 1)
                    s2_tail(bi * NB, NB, ctxT, B_ctxT)
                    if bi + 1 < nblk:
                        p_qm(bi + 1)
                s_sums(NS - 1)
                s_sel(NS - 2)
                s_ctx(NS - 1)
                s_sel(NS - 1)
                CP('act', ctxTs[:, :, :], psT[:, 0:128].rearrange("p (c s) -> p c s", c=8), [BpT], [B_ctxTs])
                s2_tail(T, NS, ctxTs, B_ctxTs)
                S.barrier()
        blocks = [(b * 512, 512) for b in range(4)] + [(T, NS)]
        cut(22)
        with ExitStack() as es3:
            hfT = sb(es3, "hfT", [128, 8, TT], BF16)
            B_hf = [Buf(f"hf{u}") for u in range(9)]

            def units(t0, n):
                return list(range(t0 // 256, (t0 + n - 1) // 256 + 1))

            mblocks = [(0, 512), (512, 512), (1024, 512), (1536, 264), (1800, 264)]
            a2 = [sb(es3, f"a2_{i}", [128, 4, TT], BF16) for i in range(2)]
            B_a2 = [[Buf() for _ in range(9)] for _ in range(2)]
            wu = [sb(es3, f"wu{i}", [128, 8, 512], BF16) for i in range(2)]
            wd = [sb(es3, f"wd{i}", [128, 4, 1024], BF16) for i in range(2)]
            B_wu, B_wd = [[Buf(), Buf()] for _ in range(2)], [[Buf(), Buf()] for _ in range(2)]
            stg = [sb(es3, f"stg3_{i}", [128, 2048], F32) for i in range(2)]
            B_stg = [Buf(), Buf()]
            rr = [sb(es3, f"rr{i}", [128, 512], F32) for i in range(2)]
            B_rr = [Buf(), Buf()]
            sq = [sb(es3, f"sq3_{i}", [128, 512], BF16) for i in range(2)]
            rstd = sb(es3, "rstd3", [128, 512], F32)
            ntmp3 = (sq, [Buf(), Buf()], rstd, Buf())
            wup_v = w_up.rearrange("(c p) f -> p c f", p=128)
            wdn_v = w_down.rearrange("(ft p) d -> p ft d", p=128)
            NG = 8
            rrk = [0]

            def load_group(fg):
                j = fg % 2
                load_w((stg, B_stg), wu[j], wup_v[:, :, fg * 512:(fg + 1) * 512], B_wu[j], 512, 8)
                load_w((stg, B_stg), wd[j], wdn_v[:, fg * 4:(fg + 1) * 4, :], B_wd[j], 1024, 4)

            def up(fg):
                j = fg % 2
                for ft in range(4):
                    for (t0, n) in mblocks:
                        ps, Bp = proj_fm(wu[j], B_wu[j], ft * 128, hfT[:, :, t0:t0 + n], [B_hf[u] for u in units(t0, n)], n)
                        k = rrk[0] % 2
                        rrk[0] += 1
                        ACT(rr[k][:, 0:n], ps[:, 0:n], AF.Relu, [Bp], [B_rr[k]])
                        ACT(a2[j][:, ft, t0:t0 + n], rr[k][:, 0:n], AF.Square, [B_rr[k]], [B_a2[j][u] for u in units(t0, n)])

            def down(fg):
                j = fg % 2
                for ct in range(8):
                    for (t0, n) in mblocks:
                        ps, Bp = nps()
                        for ft in range(4):
                            MM(ps[:, 0:n], wd[j][:, ft, ct * 128:(ct + 1) * 128], a2[j][:, ft, t0:t0 + n], ft == 0, ft == 3,
                               B_wd[j] + [B_a2[j][u] for u in units(t0, n)], Bp)
                        add_into_x(ps, Bp, ct, t0, n)

            def down_block(fg, b):
                j = fg % 2
                t0, n = blocks[b]
                for ct in range(8):
                    ps, Bp = nps()
                    for ft in range(4):
                        MM(ps[:, 0:n], wd[j][:, ft, ct * 128:(ct + 1) * 128], a2[j][:, ft, t0:t0 + n], ft == 0, ft == 3,
                           B_wd[j] + [B_a2[j][u] for u in units(t0, n)], Bp)
                    add_into_x(ps, Bp, ct, t0, n)

            yTv = [stg[i][:, :].rearrange("p (c t) -> p c t", c=8) for i in range(2)]
            wuf = [wu[i][:].rearrange("p c f -> p (c f)").bitcast(F32) for i in range(2)]
            yrow = [wuf[0][:, 0:1024], wuf[0][:, 1024:2048], wuf[1][:, 0:1024], wuf[1][:, 1024:2048]]
            B_yrw = [B_wu[0][0], B_wu[0][1], B_wu[1][0], B_wu[1][1]]
            fin_k = [0]

            def fin_norm(hb, k):
                t0, nh = hb
                rmsnorm(ntmp3, xT[:, :, t0:t0 + nh], xbufs(t0, nh), 4, yTv[k % 2][:, :, 0:nh], B_stg[k % 2], nh)

            def fin_out(hb, k):
                t0, nh = hb
                tw = min(nh, 128)
                for tt in range(max(1, nh // 128)):
                    i = fin_k[0] % 4
                    fin_k[0] += 1
                    for half in range(2):
                        ps, Bp = nps()
                        for c4 in range(4):
                            TR(ps[0:tw, c4 * 128:(c4 + 1) * 128], yTv[k % 2][:, half * 4 + c4, tt * 128:tt * 128 + tw], ident[:, :],
                               [B_stg[k % 2], B_c], Bp)
                        CP(('act', 'dve')[half], yrow[i][0:tw, half * 512:(half + 1) * 512], ps[0:tw, :], [Bp], [B_yrw[i]])
                    if nh == NS:
                        DMA('sp', y_s, yrow[i][0:tw, :], [B_yrw[i]], [], is_output=True)
                    else:
                        r0 = t0 + tt * 128
                        DMA('sp', y_p[r0:r0 + 128, :], yrow[i][:, :], [B_yrw[i]], [], is_output=True)

            load_group(0)
            load_group(1)
            for b, (t0, n) in enumerate(blocks):
                rmsnorm(ntmp3, xT[:, :, t0:t0 + n], xbufs(t0, n), 3, hfT[:, :, t0:t0 + n], [B_hf[u] for u in units(t0, n)], n)
            up(0)
            for fg in range(NG):
                if fg + 1 < NG:
                    up(fg + 1)
                if fg < NG - 1:
                    down(fg)
                else:
                    order = (4, 0, 1, 2, 3)
                    halves = []
                    for b in order:
                        t0, n = blocks[b]
                        for h0 in range(0, n, 256):
                            halves.append((b, (t0 + h0, min(256, n - h0))))
                    down_block(fg, order[0])
                    done_b = {order[0]}
                    nxt = 1
                    for k, (b, hb) in enumerate(halves):
                        while nxt < len(order) and (b not in done_b or (k + 1 < len(halves) and halves[k + 1][0] not in done_b)):
                            down_block(fg, order[nxt])
                            done_b.add(order[nxt])
                            nxt += 1
                        fin_norm(hb, k)
                        if k >= 1:
                            fin_out(halves[k - 1][1], k - 1)
                    fin_out(halves[-1][1], len(halves) - 1)
                if fg + 2 < NG:
                    load_group(fg + 2)

        S.finish('sp')
        S.emit(sem_e, sem_d)
    return nc


_CACHE = {}


def _consts():
    ident = np.eye(128, dtype=np.float32)
    j = np.arange(128)[:, None]
    i = np.arange(128)[None, :]
    tri = (j <= i).astype(np.float32)
    trirev = (j > i).astype(np.float32)
    mask4 = np.tile(tri, (1, 4)).astype(np.float32)
    e = np.zeros((16, 3, 16, 128), np.float32)
    for s in range(16):
        e[s, 0, s, :] = 1.0
        e[s, 1, s, :64] = 1.0
        e[s, 2, s, 64:] = 1.0
    sel4 = np.zeros((4, 8), np.float32)
    for c in range(8):
        sel4[c // 2, c] = 1.0
    invcnt = np.zeros((128, 4, 16), np.float32)
    for g, w in enumerate((2, 4, 8, 16)):
        for t in range(16):
            invcnt[:, g, t] = 1.0 / min(t + 1, w)
    return dict(c_ident=ident, c_tri=tri, c_trirev=trirev, c_mask4=mask4, c_e=e, c_sel4=sel4, c_invcnt=invcnt)


def kernel(x_prompt, x_sample, state_pool, state_gla, cache_mem_k, cache_mem_v, mem_prompt,
           norm_mix_g, w_in, w_forget_up, b_forget, w_pool, pool_scale, gla_norm_g, w_out,
           mem_norm_g, w_km, w_vm, norm_mem_g, w_qm, w_om, norm_ffn_g, w_up, w_down, norm_final_g):
    f = lambda a: np.ascontiguousarray(np.asarray(a, dtype=np.float32))
    if 'nc' not in _CACHE:
        _CACHE['nc'] = build_nc()
    nc = _CACHE['nc']
    wfu = np.zeros((32, 256), np.float32)
    wfu[0:16] = np.asarray(w_forget_up)[0]
    wfu[16] = np.asarray(b_forget)[0]
    gains = np.stack([np.asarray(g, np.float32).reshape(8, 128).T for g in
                      (norm_mix_g[0], mem_norm_g[0], norm_mem_g[0], norm_ffn_g[0], norm_final_g)], axis=1)
    shared = dict(
        w_in=f(w_in[0]), w_fu=wfu,
        w_pool=f(np.transpose(np.asarray(w_pool)[0], (1, 0, 2))),
        pscale=f(np.asarray(pool_scale)[0].reshape(4, 128).T),
        gla_g=f(np.asarray(gla_norm_g)[0].T),
        w_out=f(w_out[0]), w_km=f(w_km[0]), w_vm=f(w_vm[0]), w_qm=f(w_qm[0]), w_om=f(w_om[0]),
        w_up=f(w_up[0]), w_down=f(w_down[0]), gains=f(gains), **_consts())
    in_maps = []
    for c in range(NCORES):
        s0, s1 = c * NS, (c + 1) * NS
        m = dict(shared)
        m.update(xp=f(x_prompt[c]), xs=f(np.asarray(x_sample)[s0:s1, 0, :]),
                 spool=f(np.asarray(state_pool)[0, s0:s1].reshape(NS * 15, 512)),
                 sgla=f(np.asarray(state_gla)[0, s0:s1]),
                 ck=f(np.asarray(cache_mem_k)[0, s0:s1].reshape(NS, 256, 1024)),
                 cv=f(np.asarray(cache_mem_v)[0, s0:s1].reshape(NS, 256, 1024)),
                 memp=f(mem_prompt[c]))
        in_maps.append(m)
    res = run_bass_kernel_spmd(nc, in_maps, core_ids=list(range(NCORES)), **({'trace': True} if os.environ.get('K_TRACE') else {}))
    if os.environ.get('K_TRACE'):
        print('EXEC_NS', res.exec_time_ns)
    R = res.results
    y_prompt = np.stack([R[c]["y_p"] for c in range(NCORES)])
    y_sample = np.concatenate([R[c]["y_s"] for c in range(NCORES)])[:, None, :]
    pool_p = np.stack([R[c]["pool_p"] for c in range(NCORES)])[None]
    gla_p = np.stack([R[c]["gla_p"] for c in range(NCORES)])[None]
    mk_p = np.stack([R[c]["mk_p"].reshape(256, 4, 256) for c in range(NCORES)])[None]
    mv_p = np.stack([R[c]["mv_p"].reshape(256, 4, 256) for c in range(NCORES)])[None]
    pool_s = np.concatenate([R[c]["pool_s"] for c in range(NCORES)])[None]
    gla_s = np.concatenate([R[c]["gla_s"] for c in range(NCORES)])[None]
    return (y_prompt.astype(np.float32), y_sample.astype(np.float32), pool_p.astype(np.float32),
            gla_p.astype(np.float32), mk_p.astype(np.float32), mv_p.astype(np.float32),
            pool_s.astype(np.float32), gla_s.astype(np.float32))
```

```python
import numpy as np
from contextlib import ExitStack
import concourse.bass as bass
import concourse.mybir as mybir
from concourse.bass_utils import run_bass_kernel_spmd

F32 = mybir.dt.float32
BF16 = mybir.dt.bfloat16
AF = mybir.ActivationFunctionType
ALU = mybir.AluOpType
AX = mybir.AxisListType

ENGS = ('pe', 'act', 'dve', 'pool', 'sp')
NCORES = 8
D = 1024
T = 2048
NS = 16
TT = T + NS
EPS = 1e-6
import os
STOP = 99
CUT = int(os.environ.get('K_CUT', '99'))


class Buf:
    __slots__ = ('name', 'w', 'r', 'excl')

    def __init__(self, name='', excl=False):
        self.name = name
        self.w = None
        self.r = {}
        self.excl = excl


class Op:
    __slots__ = ('eng', 'fn', 'deps', 'sig', 'dma', 'sem', 'val', 'pre', 'seq', 'inc')


class Sched:
    def __init__(self, nc):
        self.nc = nc
        self.ops = {e: [] for e in ENGS}
        self.all = []
        self.out_dmas = []
        self.last = {e: None for e in ENGS}
        self.live_dmas = []
        self.dead = False

    def _mk(self, eng, fn, dma, inc):
        op = Op()
        op.eng = eng
        op.fn = fn
        op.dma = dma
        op.sig = dma
        op.inc = inc
        op.seq = len(self.all)
        op.sem = None
        op.val = 0
        op.pre = None
        op.deps = []
        return op

    def add(self, eng, fn, reads=(), writes=(), dma=False, inc=16, is_output=False):
        if self.dead:
            return None
        op = self._mk(eng, fn, dma, inc)
        deps = {}
        for b in reads:
            d = b.w
            if d is not None:
                deps[d.seq] = d
            if b.excl:
                for d in b.r.values():
                    deps[d.seq] = d
        for b in writes:
            d = b.w
            if d is not None:
                deps[d.seq] = d
            for d in b.r.values():
                deps[d.seq] = d
        for d in deps.values():
            if d is op:
                continue
            if (not d.dma) and (not dma) and d.eng == eng and eng == 'pe':
                continue
            d.sig = True
            op.deps.append(d)
        key = ('dma', op.seq) if dma else eng
        for b in reads:
            b.r[key] = op
        for b in writes:
            b.w = op
            b.r = {}
        self.ops[eng].append(op)
        self.all.append(op)
        if dma:
            self.live_dmas.append(op)
        else:
            self.last[eng] = op
        if is_output:
            self.out_dmas.append(op)
        return op

    def barrier(self):
        if self.dead:
            return
        lasts = [o for o in self.last.values() if o is not None] + list(self.live_dmas)
        for o in lasts:
            o.sig = True
        for eng in ENGS:
            op = self._mk(eng, None, False, 0)
            op.deps = [o for o in lasts if not ((not o.dma) and o.eng == eng and eng == 'pe')]
            self.ops[eng].append(op)
            self.all.append(op)
        self.live_dmas = []

    def finish(self, eng='sp'):
        op = self._mk(eng, None, False, 0)
        op.deps = list(self.out_dmas)
        self.ops[eng].append(op)
        self.all.append(op)

    def emit(self, sems_eng, sems_dma):
        cnt = {e: 0 for e in ENGS}
        rr = {e: 0 for e in ENGS}
        cur = {}
        for op in self.all:
            if op.dma:
                lst = sems_dma[op.eng]
                s = lst[rr[op.eng] % len(lst)]
                rr[op.eng] += 1
                v = cur.get(id(s), 0)
                op.pre = (s, v) if v > 0 else None
                op.sem = s
                op.val = v + op.inc
                cur[id(s)] = op.val
            elif op.sig:
                cnt[op.eng] += 1
                op.sem = sems_eng[op.eng]
                op.val = cnt[op.eng]
        sched = self

        def run(eng_name, engine):
            waited = {}
            for op in sched.ops[eng_name]:
                ws = []
                if op.pre is not None:
                    ws.append(op.pre)
                for d in op.deps:
                    ws.append((d.sem, d.val))
                for (s, v) in ws:
                    if waited.get(id(s), 0) < v:
                        engine.wait_ge(s, v)
                        waited[id(s)] = v
                if op.fn is None:
                    continue
                ins = op.fn(engine)
                if op.sig:
                    ins.then_inc(op.sem, op.inc if op.dma else 1)

        with self.nc.Block() as block:
            @block.sync
            def _(e):
                run('sp', e)

            @block.scalar
            def _(e):
                run('act', e)

            @block.vector
            def _(e):
                run('dve', e)

            @block.gpsimd
            def _(e):
                run('pool', e)

            @block.tensor
            def _(e):
                run('pe', e)


def build_nc():
    nc = bass.Bass("TRN2", target_bir_lowering=False)

    def din(name, shape, dt=F32):
        return nc.dram_tensor(name, list(shape), dt, kind="ExternalInput").ap()

    def dout(name, shape):
        return nc.dram_tensor(name, list(shape), F32, kind="ExternalOutput").ap()

    xp = din("xp", [T, D])
    xs = din("xs", [NS, D])
    spool = din("spool", [NS * 15, 512])
    sgla = din("sgla", [NS, 4, 64, 128])
    ck = din("ck", [NS, 256, 1024])
    cv = din("cv", [NS, 256, 1024])
    memp = din("memp", [256, D])
    w_in = din("w_in", [D, 2064])
    w_fu = din("w_fu", [32, 256])
    w_pool = din("w_pool", [128, 4, 128])
    pscale = din("pscale", [128, 4])
    gla_g = din("gla_g", [128, 4])
    w_out = din("w_out", [D, D])
    w_km = din("w_km", [D, D])
    w_vm = din("w_vm", [D, D])
    w_qm = din("w_qm", [D, D])
    w_om = din("w_om", [D, D])
    w_up = din("w_up", [D, 4096])
    w_down = din("w_down", [4096, D])
    gains = din("gains", [128, 5, 8])
    c_ident = din("c_ident", [128, 128])
    c_tri = din("c_tri", [128, 128])
    c_trirev = din("c_trirev", [128, 128])
    c_mask4 = din("c_mask4", [128, 512])
    c_e = din("c_e", [16, 3, 16, 128])
    c_sel4 = din("c_sel4", [4, 8])
    c_invcnt = din("c_invcnt", [128, 4, 16])

    y_p = dout("y_p", [T, D])
    y_s = dout("y_s", [NS, D])
    pool_p = dout("pool_p", [15, 512])
    gla_p = dout("gla_p", [4, 64, 128])
    mk_p = dout("mk_p", [256, 1024])
    mv_p = dout("mv_p", [256, 1024])
    pool_s = dout("pool_s", [NS, 15, 512])
    gla_s = dout("gla_s", [NS, 4, 64, 128])

    S = Sched(nc)

    def cut(k):
        if CUT == k:
            S.dead = True

    def MM(out, lhsT, rhs, start, stop, reads, wb):
        S.add('pe', lambda e: e.matmul(out, lhsT=lhsT, rhs=rhs, start=start, stop=stop), reads=reads, writes=[wb])

    def TR(out, in_, ident, reads, wb):
        S.add('pe', lambda e: e.transpose(out, in_, ident), reads=reads, writes=[wb])

    def ACT(out, in_, func, reads, writes, **kw):
        S.add('act', lambda e: e.activation(out=out, in_=in_, func=func, **kw), reads=reads, writes=writes)

    def DMA(q, out, in_, reads, writes, is_output=False):
        S.add(q, lambda e: e.dma_start(out=out, in_=in_), reads=reads, writes=writes, dma=True, is_output=is_output)

    def TT_(eng, out, in0, in1, op, reads, writes):
        S.add(eng, lambda e: e.tensor_tensor(out=out, in0=in0, in1=in1, op=op), reads=reads, writes=writes)

    def TS(eng, out, in0, s1, op0, reads, writes, s2=None, op1=None):
        if op1 is None:
            S.add(eng, lambda e: e.tensor_scalar(out=out, in0=in0, scalar1=s1, scalar2=None, op0=op0), reads=reads, writes=writes)
        else:
            S.add(eng, lambda e: e.tensor_scalar(out=out, in0=in0, scalar1=s1, scalar2=s2, op0=op0, op1=op1), reads=reads, writes=writes)

    def STT(out, in0, scalar, in1, op0, op1, reads, writes):
        S.add('dve', lambda e: e.scalar_tensor_tensor(out=out, in0=in0, scalar=scalar, in1=in1, op0=op0, op1=op1), reads=reads, writes=writes)

    def CP(eng, out, in_, reads, writes):
        if eng == 'act':
            S.add('act', lambda e: e.copy(out=out, in_=in_), reads=reads, writes=writes)
        else:
            S.add(eng, lambda e: e.tensor_copy(out=out, in_=in_), reads=reads, writes=writes)

    with ExitStack() as top:
        sem_e = {e: top.enter_context(nc.semaphore(f"s_{e}")) for e in ('pe', 'act', 'dve', 'pool')}
        sem_d = {q: [top.enter_context(nc.semaphore(f"d_{q}{i}")) for i in range(10)] for q in ('sp', 'act', 'pool')}
        PS = [top.enter_context(nc.psum_tensor(f"ps{i}", [128, 512], F32)) for i in range(8)]
        B_PS = [Buf(f"ps{i}", excl=True) for i in range(8)]
        ps_rr = [0]

        PSR = [PS[6], PS[7]]
        B_PSR = [B_PS[6], B_PS[7]]

        ps_lo, ps_n = [0], [6]

        def nps():
            i = ps_lo[0] + ps_rr[0] % ps_n[0]
            ps_rr[0] += 1
            return PS[i], B_PS[i]

        def sb(es, name, shape, dt):
            return es.enter_context(nc.sbuf_tensor(name, list(shape), dt))

        xT = sb(top, "xT", [128, 8, TT], F32)
        B_xT = [[Buf(f"xT{c}_{b}") for b in range(9)] for c in range(8)]

        def xbufs(t0, n, chunks=range(8)):
            out = []
            for c in chunks:
                for b in range(t0 // 256, (t0 + n - 1) // 256 + 1):
                    out.append(B_xT[c][b])
            return out

        mkT = sb(top, "mkT", [128, 8, 256], BF16)
        mvb = sb(top, "mvb", [128, 2, 1024], BF16)
        B_mkT, B_mvb = Buf("mkT"), Buf("mvb")
        ident = sb(top, "ident", [128, 128], F32)
        ones_bf = sb(top, "ones_bf", [128, 128], BF16)
        gn = sb(top, "gn", [128, 5, 8], F32)
        B_c = Buf("consts")
        DMA('sp', ident[:], c_ident, [], [B_c])
        DMA('sp', gn[:], gains, [], [B_c])
        S.add('dve', lambda e: e.memset(ones_bf[:], 1.0), writes=[B_c])

        stage_rr = [0]

        def load_w(es_bufs, dst3, src3, B_dst, ncols, nch):
            stg, B_stg = es_bufs
            per = max(1, 2048 // ncols)
            c = 0
            k = 0
            while c < nch:
                m = min(per, nch - c)
                i = stage_rr[0] % len(stg)
                stage_rr[0] += 1
                sv = stg[i][:, 0:m * ncols].rearrange("p (a b) -> p a b", a=m)
                DMA('sp', sv, src3[:, c:c + m, :], [], [B_stg[i]])
                eng = ('act', 'dve')[k % 2]
                Bd = B_dst[k] if isinstance(B_dst, list) else B_dst
                CP(eng, dst3[:, c:c + m, :], sv, [B_stg[i]], [Bd])
                c += m
                k += 1

        def rmsnorm(tmp, src3, B_src, gidx, dst3, B_dst, n, nfeat=1024.0):
            sq, B_sq, rstd, B_rstd = tmp
            ps, B_ps = nps()
            for c in range(8):
                j = c % 2
                ACT(sq[j][:, 0:n], src3[:, c, :], AF.Square, B_src, [B_sq[j]])
                MM(ps[:, 0:n], ones_bf[:], sq[j][:, 0:n], c == 0, c == 7, [B_sq[j], B_c], B_ps)
            ACT(rstd[:, 0:n], ps[:, 0:n], AF.Ln, [B_ps], [B_rstd], scale=1.0 / nfeat, bias=EPS)
            ACT(rstd[:, 0:n], rstd[:, 0:n], AF.Exp, [B_rstd], [B_rstd], scale=-0.5)
            for c in range(8):
                STT(dst3[:, c, :], src3[:, c, :], gn[:, gidx, c:c + 1], rstd[:, 0:n], ALU.mult, ALU.mult,
                    list(B_src) + [B_rstd, B_c], (list(B_dst) if isinstance(B_dst, (list, tuple)) else [B_dst]))

        def proj_fm(wbf, B_w, col0, hT3, B_h, n, nk=8, mcols=128):
            ps, B_ps = nps()
            for k in range(nk):
                MM(ps[0:mcols, 0:n], wbf[:, k, col0:col0 + mcols], hT3[:, k, :], k == 0, k == nk - 1,
                   (list(B_w) if isinstance(B_w, (list, tuple)) else [B_w])
                   + (list(B_h) if isinstance(B_h, (list, tuple)) else [B_h]), B_ps)
            return ps, B_ps

        def proj_tm(wbf, B_w, col0, ncols, hT3, B_h, tok0, ntok, nk=8):
            ps, B_ps = nps()
            for k in range(nk):
                MM(ps[0:ntok, 0:ncols], hT3[:, k, tok0:tok0 + ntok], wbf[:, k, col0:col0 + ncols], k == 0, k == nk - 1,
                   (list(B_w) if isinstance(B_w, (list, tuple)) else [B_w]) + [B_h], B_ps)
            return ps, B_ps

        def add_into_x(ps, B_ps, ct, t0, n):
            xv = xT[:, ct, t0:t0 + n]
            bl = xbufs(t0, n, [ct])
            TT_('dve', xv, ps[:, 0:n], xv, ALU.add, [B_ps] + bl, bl)

        with ExitStack() as es:
            stg = [sb(es, f"stg{i}", [128, 2048], F32) for i in range(4)]
            B_stg = [Buf(f"stg{i}") for i in range(4)]
            xrow = [sb(es, f"xrow{i}", [128, 1024], F32) for i in range(4)]
            B_xrow = [Buf(f"xrow{i}") for i in range(4)]
            memT = sb(es, "memT", [128, 8, 256], F32)
            B_memT = Buf("memT")
            mnT = sb(es, "mnT", [128, 8, 256], BF16)
            B_mnT = Buf("mnT")
            wkm = sb(es, "wkm", [128, 8, 1024], BF16)
            wvm = sb(es, "wvm", [128, 8, 1024], BF16)
            B_wkm, B_wvm = [Buf() for _ in range(4)], [Buf() for _ in range(4)]
            mkf = sb(es, "mkf", [128, 2, 1024], F32)
            mvf = sb(es, "mvf", [128, 2, 1024], F32)
            B_mkf, B_mvf = Buf("mkf"), Buf("mvf")
            sq = [sb(es, f"sq{i}", [128, 512], BF16) for i in range(2)]
            B_sq = [Buf(), Buf()]
            rstd = sb(es, "rstd", [128, 512], F32)
            B_rstd = Buf()
            ntmp = (sq, B_sq, rstd, B_rstd)

            def load_rows_T(src_rows, nrows, dstT, t0, bufs_dst, i):
                DMA('sp', xrow[i][0:nrows, :], src_rows, [], [B_xrow[i]])
                for half in range(2):
                    ps, B_ps = nps()
                    for c4 in range(4):
                        c = half * 4 + c4
                        TR(ps[:, c4 * 128:c4 * 128 + nrows], xrow[i][0:nrows, c * 128:(c + 1) * 128], ident[0:nrows, 0:nrows],
                           [B_xrow[i], B_c], B_ps)
                    eng = ('act', 'dve')[half]
                    CP(eng, dstT[:, half * 4:half * 4 + 4, t0:t0 + nrows],
                       ps[:, :].rearrange("p (a b) -> p a b", a=4)[:, :, 0:nrows], [B_ps], bufs_dst(half))

            for tt in range(16):
                load_rows_T(xp[tt * 128:(tt + 1) * 128, :], 128, xT, tt * 128,
                            lambda half, tt=tt: xbufs(tt * 128, 128, range(half * 4, half * 4 + 4)), tt % 4)
            load_rows_T(xs[:, :], NS, xT, T, lambda half: xbufs(T, NS, range(half * 4, half * 4 + 4)), 0)
            for mt in range(2):
                load_rows_T(memp[mt * 128:(mt + 1) * 128, :], 128, memT, mt * 128, lambda half: [B_memT], (mt + 1) % 4)
            if CUT >= 2:
                rmsnorm(ntmp, memT, [B_memT], 1, mnT, B_mnT, 256)
            wkm_v = w_km.rearrange("(c p) f -> p c f", p=128)
            wvm_v = w_vm.rearrange("(c p) f -> p c f", p=128)
            if CUT >= 3:
                load_w((stg, B_stg), wkm, wkm_v, B_wkm, 1024, 8)
                load_w((stg, B_stg), wvm, wvm_v, B_wvm, 1024, 8)
            for mt in range(2 if CUT >= 4 else 0):
                for half in range(2):
                    ps, B_ps = proj_tm(wkm, B_wkm, half * 512, 512, mnT, B_mnT, mt * 128, 128)
                    CP('act', mkf[:, mt, half * 512:(half + 1) * 512], ps[:, :], [B_ps], [B_mkf])
                    ps, B_ps = proj_tm(wvm, B_wvm, half * 512, 512, mnT, B_mnT, mt * 128, 128)
                    CP('act', mvf[:, mt, half * 512:(half + 1) * 512], ps[:, :], [B_ps], [B_mvf])
                    CP('dve', mvb[:, mt, half * 512:(half + 1) * 512], ps[:, :], [B_ps], [B_mvb])
            DMA('sp', mk_p.rearrange("(a p) f -> p a f", p=128), mkf[:], [B_mkf], [], is_output=True)
            DMA('sp', mv_p.rearrange("(a p) f -> p a f", p=128), mvf[:], [B_mvf], [], is_output=True)
            for c in range(8 if CUT >= 5 else 0):
                ps, B_ps = proj_fm(wkm, B_wkm, c * 128, mnT, B_mnT, 256)
                CP(('act', 'dve')[c % 2], mkT[:, c, :], ps[:, 0:256], [B_ps], [B_mkT])
            S.barrier()

        BLK = 256
        cut(9)
        with ExitStack() as es1:
            win = sb(es1, "win", [128, 8, 2064], BF16)
            wout = sb(es1, "wout", [128, 8, 1024], BF16)
            wpool = sb(es1, "wpool", [128, 4, 128], BF16)
            wfu = sb(es1, "wfu", [32, 256], BF16)
            psc = sb(es1, "psc", [128, 4], F32)
            glag = sb(es1, "glag", [128, 4], F32)
            tri = sb(es1, "tri", [128, 128], F32)
            trirev = sb(es1, "trirev", [128, 128], F32)
            mask4 = sb(es1, "mask4", [128, 512], F32)
            invc = sb(es1, "invc", [128, 4, 16], F32)
            onesLH = sb(es1, "onesLH", [16, 2, 128], BF16)
            B_win_a, B_win_b, B_wout, B_k1 = [Buf() for _ in range(8)], [Buf() for _ in range(8)], [Buf() for _ in range(4)], Buf("k1")
            B_win = B_win_a + B_win_b
            hT = sb(es1, "hT", [128, 8, BLK], BF16)
            sq = [sb(es1, f"sq1_{i}", [128, 512], BF16) for i in range(2)]
            rstd = sb(es1, "rstd1", [128, 512], F32)
            B_hT, B_sq, B_rstd = Buf("hT"), [Buf(), Buf()], Buf()
            ntmp = (sq, B_sq, rstd, B_rstd)
            flT = sb(es1, "flT", [32, BLK], BF16)
            ez = sb(es1, "ez", [128, 256], F32)
            sp = sb(es1, "sp", [128, 2, 256], F32)
            sg = sb(es1, "sg", [128, 4, BLK], F32)
            uT = sb(es1, "uT", [128, 4, 16 + BLK], F32)
            tA = sb(es1, "tA", [128, 16 + BLK], F32)
            tB = sb(es1, "tB", [128, 16 + BLK], F32)
            pooled = sb(es1, "pooled", [128, 4, BLK], BF16)
            of32 = sb(es1, "of32", [128, 4, BLK], F32)
            osq = sb(es1, "osq", [128, 4, BLK], BF16)
            rst2 = sb(es1, "rst2", [128, BLK], F32)
            t1 = sb(es1, "t1", [128, BLK], F32)
            mixedT = sb(es1, "mixedT", [128, 8, BLK], BF16)
            B_fl, B_ez, B_sp, B_sg, B_uT, B_tA, B_tB = Buf(), Buf(), Buf(), Buf(), Buf(), Buf(), Buf()
            B_pooled, B_of, B_osq, B_rst2, B_t1, B_mixed = Buf(), Buf(), Buf(), Buf(), Buf(), Buf()

            with ExitStack() as esl:
                stg = [sb(esl, f"stg1_{i}", [128, 2048], F32) for i in range(4)]
                B_stg = [Buf() for _ in range(4)]
                ctmp = sb(esl, "ctmp", [128, 768], F32)
                B_ctmp = Buf()
                win_v = w_in.rearrange("(c p) f -> p c f", p=128)
                load_w((stg, B_stg), win[:, :, 0:1032], win_v[:, :, 0:1032], B_win_a, 1032, 8)
                load_w((stg, B_stg), win[:, :, 1032:2064], win_v[:, :, 1032:2064], B_win_b, 1032, 8)
                load_w((stg, B_stg), wout, w_out.rearrange("(c p) f -> p c f", p=128), B_wout, 1024, 8)
                DMA('sp', ctmp[:, 0:512], w_pool.rearrange("p g d -> p (g d)"), [], [B_ctmp])
                DMA('sp', ctmp[0:32, 512:768], w_fu, [], [B_ctmp])
                CP('dve', wpool[:].rearrange("p g d -> p (g d)"), ctmp[:, 0:512], [B_ctmp], [B_k1])
                CP('dve', wfu[:], ctmp[0:32, 512:768], [B_ctmp], [B_k1])
                for (dst, src) in ((psc, pscale), (glag, gla_g), (tri, c_tri), (trirev, c_trirev), (mask4, c_mask4),
                                   (invc, c_invcnt)):
                    DMA('sp', dst[:], src, [], [B_k1])
                S.add('dve', lambda e: e.memset(onesLH[:], 0.0), writes=[B_k1])
                S.add('dve', lambda e: e.memset(onesLH[:, 0, 0:64], 1.0), writes=[B_k1])
                S.add('dve', lambda e: e.memset(onesLH[:, 1, 64:128], 1.0), writes=[B_k1])
                S.add('dve', lambda e: e.memset(flT[:], 1.0), writes=[B_fl])
                S.add('dve', lambda e: e.memset(uT[:], 0.0), writes=[B_uT])
                S.barrier()
            cut(10)

            def s1_common(t0, n):
                h3 = hT[:, :, 0:n]
                rmsnorm(ntmp, xT[:, :, t0:t0 + n], xbufs(t0, n), 0, h3, B_hT, n)
                ps, Bp = proj_fm(win, B_win, 2048, h3, B_hT, n, mcols=16)
                CP('act', flT[0:16, 0:n], ps[0:16, 0:n], [Bp], [B_fl])
                ntt = max(1, n // 128)
                tw = min(n, 128)
                for tt in range(ntt):
                    ps, Bp = nps()
                    MM(ps[0:tw, 0:256], flT[0:32, tt * 128:tt * 128 + tw], wfu[0:32, :], True, True, [B_fl, B_k1], Bp)
                    ACT(ez[0:tw, :], ps[0:tw, 0:256], AF.Exp, [Bp], [B_ez], scale=-1.0)
                    ACT(sp[0:tw, tt, :], ez[0:tw, :], AF.Ln, [B_ez], [B_sp], bias=1.0)
                for hh in range(4):
                    ps, Bp = proj_fm(win, B_win, 1536 + hh * 128, h3, B_hT, n)
                    ACT(sg[:, hh, 0:n], ps[:, 0:n], AF.Silu, [Bp], [B_sg])
                for g in range(4):
                    ps, Bp = proj_fm(win, B_win, g * 128, h3, B_hT, n)
                    CP('act', uT[:, g, 16:16 + n], ps[:, 0:n], [Bp], [B_uT])
                return h3

            def s1_tail(t0, n):
                for g in range(4):
                    ps, Bp = nps()
                    MM(ps[:, 0:n], wpool[:, g, :], pooled[:, g, 0:n], True, True, [B_k1, B_pooled], Bp)
                    TS('dve', mixedT[:, g, 0:n], ps[:, 0:n], psc[:, g:g + 1], ALU.mult, [Bp, B_k1], [B_mixed])
                ACT(osq[:, :, 0:n], of32[:, :, 0:n], AF.Square, [B_of], [B_osq])
                for h in range(4):
                    ps, Bp = nps()
                    MM(ps[:, 0:n], ones_bf[:], osq[:, h, 0:n], True, True, [B_osq, B_c], Bp)
                    ACT(rst2[:, 0:n], ps[:, 0:n], AF.Ln, [Bp], [B_rst2], scale=1.0 / 128.0, bias=EPS)
                    ACT(rst2[:, 0:n], rst2[:, 0:n], AF.Exp, [B_rst2], [B_rst2], scale=-0.5)
                    STT(t1[:, 0:n], of32[:, h, 0:n], glag[:, h:h + 1], rst2[:, 0:n], ALU.mult, ALU.mult,
                        [B_of, B_k1, B_rst2], [B_t1])
                    TT_('dve', mixedT[:, 4 + h, 0:n], t1[:, 0:n], sg[:, h, 0:n], ALU.mult, [B_t1, B_sg], [B_mixed])
                for ct in range(8):
                    ps, Bp = proj_fm(wout, B_wout, ct * 128, mixedT[:, :, 0:n], B_mixed, n)
                    add_into_x(ps, Bp, ct, t0, n)

            with ExitStack() as esa:
                S0 = sb(esa, "S0", [128, NS, 2, 128], F32)
                Sbf = sb(esa, "Sbf", [128, NS, 2, 128], BF16)
                stT = sb(esa, "stT", [128, 4, 240], F32)
                rowb = [sb(esa, f"rowb{i}", [128, 512], F32) for i in range(2)]
                decT = sb(esa, "decT", [128, 2, NS], F32)
                qTs = [sb(esa, f"qTs{r}", [128, 2, NS], BF16) for r in range(2)]
                kTs = sb(esa, "kTs", [128, 2, NS], F32)
                vs_bf = sb(esa, "vs_bf", [16, 512], BF16)
                us_f = sb(esa, "us_f", [16, 512], F32)
                vm = [sb(esa, f"vm{i}", [16, 512], BF16) for i in range(2)]
                wsum = sb(esa, "wsum", [128, NS], F32)
                B_S0, B_Sbf, B_stT, B_rowb = [Buf() for _ in range(NS)], Buf(), Buf(), [Buf(), Buf()]
                B_dec, B_qTs, B_kTs, B_vs, B_us, B_vm, B_ws = Buf(), Buf(), Buf(), Buf(), Buf(), [Buf(), Buf()], Buf()
                t0, n = T, NS
                for r in range(2):
                    S.add('dve', lambda e, r=r: e.memset(qTs[r][:], 0.0), writes=[B_qTs])
                for s in range(NS):
                    DMA('sp', S0[:, s, :, :], sgla[s].rearrange("(hp hr) k v -> (hr k) hp v", hr=2), [], [B_S0[s]])
                for i, (r0, nr) in enumerate(((0, 128), (128, 112))):
                    DMA('sp', rowb[i][0:nr, :], spool[r0:r0 + nr, :], [], [B_rowb[i]])
                    ps, Bp = nps()
                    for g in range(4):
                        TR(ps[:, g * 128:g * 128 + nr], rowb[i][0:nr, g * 128:(g + 1) * 128], ident[0:nr, 0:nr],
                           [B_rowb[i], B_c], Bp)
                    CP('act', stT[:, :, r0:r0 + nr], ps[:, :].rearrange("p (g r) -> p g r", g=4)[:, :, 0:nr], [Bp], [B_stT])
                DMA('sp', pool_s[:, 0:14, :], spool.rearrange("(s r) c -> s r c", r=15)[:, 1:15, :], [], [], is_output=True)
                h3 = s1_common(t0, n)
                ps, Bp = nps()
                for p in range(2):
                    TR(ps[:, p * 16:(p + 1) * 16], sp[0:16, 0, p * 128:(p + 1) * 128], ident[0:16, 0:16], [B_sp, B_c], Bp)
                ACT(decT[:].rearrange("p a s -> p (a s)"), ps[:, 0:32], AF.Exp, [Bp], [B_dec], scale=-1.0 / 16.0)
                for p in range(2):
                    ps, Bp = proj_fm(win, B_win, 512 + p * 128, h3, B_hT, n)
                    for r in range(2):
                        ACT(qTs[r][r * 64:(r + 1) * 64, p, :], ps[r * 64:(r + 1) * 64, 0:n], AF.Copy, [Bp], [B_qTs], scale=0.125)
                    ps, Bp = proj_fm(win, B_win, 768 + p * 128, h3, B_hT, n)
                    CP('dve', kTs[:, p, :], ps[:, 0:n], [Bp], [B_kTs])
                ps, Bp = proj_tm(win, B_win, 1024, 512, h3, B_hT, 0, n)
                CP('act', vs_bf[:, :], ps[0:n, :], [Bp], [B_vs])
                ps, Bp = proj_tm(win, B_win, 0, 512, h3, B_hT, 0, n)
                CP('act', us_f[:, :], ps[0:n, :], [Bp], [B_us])
                DMA('sp', pool_s[:, 14, :], us_f[:, :], [B_us], [], is_output=True)
                for g, w in enumerate((2, 4, 8, 16)):
                    st3 = stT[:, g, :].rearrange("p (s r) -> p s r", r=15)[:, :, 16 - w:15]
                    S.add('dve', lambda e, st3=st3: e.tensor_reduce(out=wsum[:, :], in_=st3, axis=AX.X, op=ALU.add),
                          reads=[B_stT], writes=[B_ws])
                    TT_('dve', wsum[:, :], wsum[:, :], uT[:, g, 16:16 + n], ALU.add, [B_ws, B_uT], [B_ws])
                    STT(pooled[:, g, 0:n], wsum[:, :], 1.0 / w, uT[:, g, 16:16 + n], ALU.mult, ALU.subtract,
                        [B_ws, B_uT], [B_pooled])
                for hp in range(2):
                    TT_('dve', S0[:, :, hp, :], S0[:, :, hp, :],
                        decT[:, hp, :].unsqueeze(2).broadcast_to([128, NS, 128]), ALU.mult,
                        B_S0 + [B_dec], B_S0)
                for s in range(NS):
                    i = s % 2
                    ACT(vm[i][:, :], vs_bf[:, :], AF.Copy, [B_vs, B_c], [B_vm[i]], scale=ident[0:16, s:s + 1])
                    if i == 0:
                        psV, BpV = nps()
                    vm4 = vm[i][:, :].rearrange("s (hp hr v) -> s hp hr v", hp=2, hr=2)
                    ov = psV[:, i * 256:(i + 1) * 256].rearrange("p (a v) -> p a v", a=2)
                    MM(ov, onesLH[:, 0, :], vm4[:, :, 0, :], True, False, [B_k1, B_vm[i]], BpV)
                    MM(ov, onesLH[:, 1, :], vm4[:, :, 1, :], False, True, [B_k1, B_vm[i]], BpV)
                    for hp in range(2):
                        STT(S0[:, s, hp, :], psV[:, i * 256 + hp * 128:i * 256 + (hp + 1) * 128], kTs[:, hp, s:s + 1],
                            S0[:, s, hp, :], ALU.mult, ALU.add, [BpV, B_kTs, B_S0[s]], [B_S0[s]])
                    DMA('sp', gla_s[s].rearrange("(hp hr) k v -> (hr k) hp v", hr=2), S0[:, s, :, :], [B_S0[s]], [],
                        is_output=True)
                CP('act', Sbf[:, 0:8], S0[:, 0:8], B_S0, [B_Sbf])
                CP('dve', Sbf[:, 8:16], S0[:, 8:16], B_S0, [B_Sbf])
                psO, BpO = nps()
                for s in range(NS):
                    for h in range(4):
                        p, r = h // 2, h % 2
                        MM(psO[:, h * 16 + s:h * 16 + s + 1], Sbf[:, s, p, :], qTs[r][:, p, s:s + 1], True, True,
                           [B_Sbf, B_qTs], BpO)
                CP('act', of32[:, :, 0:n], psO[:, 0:64].rearrange("d (h s) -> d h s", h=4), [BpO], [B_of])
                s1_tail(t0, n)
                S.add('dve', lambda e: e.memset(uT[:], 0.0), reads=[B_uT], writes=[B_uT])
                S.barrier()
            cut(11)

            with ExitStack() as esb:
                epos = [sb(esb, f"epos{i}", [128, 2, BLK], F32) for i in range(2)]
                eneg = sb(esb, "eneg", [128, 2, BLK], F32)
                eend = sb(esb, "eend", [128, 2, 256], F32)
                qdT = [sb(esb, f"qdT{i}", [128, 2, BLK], BF16) for i in range(2)]
                kiT = [[sb(esb, f"kiT{i}_{r}", [128, 2, BLK], BF16) for r in range(2)] for i in range(2)]
                kend = [sb(esb, f"kend{i}", [128, 2, 256], BF16) for i in range(2)]
                vbf = [sb(esb, f"vbf{i}", [128, 2, 512], BF16) for i in range(2)]
                sg2 = [sg, sb(esb, "sg_b", [128, 4, BLK], F32)]
                pooled2 = [pooled, sb(esb, "pooled_b", [128, 4, BLK], BF16)]
                scT = sb(esb, "scT", [128, 512], BF16)
                Sf = sb(esb, "Sf", [128, 2, 128], F32)
                Sb = [sb(esb, f"Sb{r}", [128, 2, 128], BF16) for r in range(2)]
                utm = sb(esb, "utm", [128, 512], F32)
                tm16 = sb(esb, "tm16", [128, 16], F32)
                B_epos, B_eneg, B_eend = [Buf(), Buf()], Buf(), Buf()
                B_qd, B_ki, B_kend, B_v = [Buf(), Buf()], [Buf(), Buf()], [Buf(), Buf()], [Buf(), Buf()]
                B_sg2, B_pooled2 = [B_sg, Buf()], [B_pooled, Buf()]
                B_sc, B_Sf, B_Sb, B_utm, B_tm16 = Buf(), Buf(), Buf(), Buf(), Buf()
                for i in range(2):
                    for r in range(2):
                        S.add('dve', lambda e, i=i, r=r: e.memset(kiT[i][r][:], 0.0), writes=[B_ki[i]])
                for r in range(2):
                    S.add('dve', lambda e, r=r: e.memset(Sb[r][:], 0.0), writes=[B_Sb])
                NBLK = T // BLK
                ntt = BLK // 128
                n = BLK
                h3 = hT[:, :, 0:n]

                def front_steps(bi):
                    t0 = bi * BLK
                    pb = bi % 2
                    st = {}

                    def f_norm():
                        rmsnorm(ntmp, xT[:, :, t0:t0 + n], xbufs(t0, n), 0, h3, B_hT, n)

                    def f_fl():
                        ps, Bp = proj_fm(win, B_win, 2048, h3, B_hT, n, mcols=16)
                        CP('act', flT[0:16, 0:n], ps[0:16, 0:n], [Bp], [B_fl])
                        for tt in range(ntt):
                            ps, Bp = nps()
                            MM(ps[:, 0:256], flT[0:32, tt * 128:(tt + 1) * 128], wfu[0:32, :], True, True, [B_fl, B_k1], Bp)
                            ACT(ez[:, :], ps[:, 0:256], AF.Exp, [Bp], [B_ez], scale=-1.0)
                            ACT(sp[:, tt, :], ez[:, :], AF.Ln, [B_ez], [B_sp], bias=1.0)

                    def f_gate(hh):
                        ps, Bp = proj_fm(win, B_win, 1536 + hh * 128, h3, B_hT, n)
                        ACT(sg2[pb][:, hh, 0:n], ps[:, 0:n], AF.Silu, [Bp], [B_sg2[pb]])

                    def f_u(g):
                        ps, Bp = proj_fm(win, B_win, g * 128, h3, B_hT, n)
                        CP('act', uT[:, g, 16:16 + n], ps[:, 0:n], [Bp], [B_uT])

                    def f_cr():
                        for p in range(2):
                            psC, BpC = nps()
                            for tt in range(ntt):
                                MM(psC[:, tt * 128:(tt + 1) * 128], sp[:, tt, p * 128:(p + 1) * 128], tri[:], True, True,
                                   [B_sp, B_k1], BpC)
                            ACT(epos[pb][:, p, 0:n], psC[:, 0:n], AF.Exp, [BpC], [B_epos[pb]], scale=-1.0 / 16.0)
                            ACT(eneg[:, p, 0:n], psC[:, 0:n], AF.Exp, [BpC], [B_eneg], scale=1.0 / 16.0)
                        psR, BpR = nps()
                        for tt in range(ntt):
                            MM(psR[:, tt * 256:(tt + 1) * 256], trirev[:], sp[:, tt, :], True, True, [B_sp, B_k1], BpR)
                        ACT(eend[:].rearrange("p a b -> p (a b)"), psR[:, 0:512], AF.Exp, [BpR], [B_eend], scale=-1.0 / 16.0)

                    def f_qk(p):
                        ps, Bp = proj_fm(win, B_win, 512 + p * 128, h3, B_hT, n)
                        STT(qdT[pb][:, p, 0:n], ps[:, 0:n], 0.125, epos[pb][:, p, 0:n], ALU.mult, ALU.mult,
                            [Bp, B_epos[pb]], [B_qd[pb]])
                        ps, Bp = proj_fm(win, B_win, 768 + p * 128, h3, B_hT, n)
                        for r in range(2):
                            TT_('dve', kiT[pb][r][r * 64:(r + 1) * 64, p, 0:n], ps[r * 64:(r + 1) * 64, 0:n],
                                eneg[r * 64:(r + 1) * 64, p, 0:n], ALU.mult, [Bp, B_eneg], [B_ki[pb]])

                    def f_kv(tt):
                        ps, Bp = proj_tm(win, B_win, 1024, 512, h3, B_hT, tt * 128, 128)
                        CP('act', vbf[pb][:, tt, :], ps[:, :], [Bp], [B_v[pb]])
                        ps, Bp = proj_tm(win, B_win, 768, 256, h3, B_hT, tt * 128, 128)
                        TT_('dve', kend[pb][:, tt, :], ps[:, 0:256], eend[:, tt, :], ALU.mult, [Bp, B_eend], [B_kend[pb]])

                    def f_utm():
                        ps, Bp = proj_tm(win, B_win, 0, 512, h3, B_hT, (ntt - 1) * 128, 128)
                        CP('act', utm[:, :], ps[:, :], [Bp], [B_utm])
                        DMA('sp', pool_p, utm[113:128, :], [B_utm], [], is_output=True)

                    def f_pool():
                        W = 16 + n
                        for g, w in enumerate((2, 4, 8, 16)):
                            a = uT[:, g, 0:W]
                            src, Bsrc = a, B_uT
                            sh = 1
                            k = 0
                            while sh < w:
                                dst, Bdst = ((tA, B_tA), (tB, B_tB))[k % 2]
                                lo = 2 * sh - 1
                                TT_('pool', dst[:, lo:W], src[:, lo:W], src[:, lo - sh:W - sh], ALU.add, [Bsrc], [Bdst])
                                src, Bsrc = dst, Bdst
                                sh *= 2
                                k += 1
                            STT(pooled2[pb][:, g, 0:n], src[:, 16:W], 1.0 / w, a[:, 16:W], ALU.mult, ALU.subtract,
                                [Bsrc, B_uT], [B_pooled2[pb]])
                            if bi == 0:
                                TT_('dve', tm16[:, :], src[:, 16:32], invc[:, g, :], ALU.mult, [Bsrc, B_k1], [B_tm16])
                                TT_('dve', pooled2[pb][:, g, 0:16], tm16[:, :], a[:, 16:32], ALU.subtract,
                                    [B_tm16, B_uT], [B_pooled2[pb]])
                        CP('pool', uT[:, :, 0:16], uT[:, :, n:n + 16], [B_uT, B_tA, B_tB], [B_uT])

                    lst = [f_norm, f_fl, lambda: f_gate(0), lambda: f_gate(1), f_cr, lambda: f_gate(2), lambda: f_gate(3),
                           lambda: f_u(0), lambda: f_u(1), lambda: f_qk(0), lambda: f_u(2), lambda: f_qk(1), lambda: f_u(3),
                           lambda: f_kv(0), lambda: f_kv(1)]
                    if bi == NBLK - 1:
                        lst.append(f_utm)
                    lst.append(f_pool)
                    return lst

                def gla_steps(bi):
                    pb = bi % 2
                    lst = []
                    for tt in range(ntt):
                        tok = slice(tt * 128, (tt + 1) * 128)
                        first = (bi == 0 and tt == 0)
                        hold = {}

                        def g_scores(tok=tok):
                            psS, BpS = nps()
                            for h in range(4):
                                p, r = h // 2, h % 2
                                MM(psS[:, h * 128:(h + 1) * 128], kiT[pb][r][:, p, tok], qdT[pb][:, p, tok], True, True,
                                   [B_ki[pb], B_qd[pb]], BpS)
                            TT_('dve', scT[:, :], psS[:, :], mask4[:, :], ALU.mult, [BpS, B_k1], [B_sc])

                        def g_o(tok=tok, tt=tt, first=first):
                            psO, BpO = nps()
                            for h in range(4):
                                p, r = h // 2, h % 2
                                MM(psO[:, h * 128:(h + 1) * 128], vbf[pb][:, tt, h * 128:(h + 1) * 128],
                                   scT[:, h * 128:(h + 1) * 128], True, first, [B_v[pb], B_sc], BpO)
                                if not first:
                                    MM(psO[:, h * 128:(h + 1) * 128], Sb[r][:, p, :], qdT[pb][:, p, tok], False, True,
                                       [B_Sb, B_qd[pb]], BpO)
                            CP('act', of32[:, :, tok], psO[:, :].rearrange("d (h t) -> d h t", h=4), [BpO], [B_of])

                        def g_state(tt=tt, first=first):
                            psU, BpU = nps()
                            for p in range(2):
                                MM(psU[:, p * 256:(p + 1) * 256], kend[pb][:, tt, p * 128:(p + 1) * 128],
                                   vbf[pb][:, tt, p * 256:(p + 1) * 256], True, True, [B_kend[pb], B_v[pb]], BpU)
                            for h in range(4):
                                p, r = h // 2, h % 2
                                P0, P1 = r * 64, (r + 1) * 64
                                uu = psU[P0:P1, p * 256 + r * 128:p * 256 + (r + 1) * 128]
                                if first:
                                    CP('dve', Sf[P0:P1, p, :], uu, [BpU], [B_Sf])
                                else:
                                    STT(Sf[P0:P1, p, :], Sf[P0:P1, p, :], epos[pb][P0:P1, p, tt * 128 + 127:tt * 128 + 128], uu,
                                        ALU.mult, ALU.add, [B_Sf, B_epos[pb], BpU], [B_Sf])
                            for r in range(2):
                                CP('act', Sb[r][r * 64:(r + 1) * 64, :, :], Sf[r * 64:(r + 1) * 64, :, :], [B_Sf], [B_Sb])

                        lst += [g_scores, g_o, g_state]
                    return lst

                def tail_steps(bi):
                    t0 = bi * BLK
                    pb = bi % 2

                    def t_pool():
                        for g in range(4):
                            ps, Bp = nps()
                            MM(ps[:, 0:n], wpool[:, g, :], pooled2[pb][:, g, 0:n], True, True, [B_k1, B_pooled2[pb]], Bp)
                            TS('dve', mixedT[:, g, 0:n], ps[:, 0:n], psc[:, g:g + 1], ALU.mult, [Bp, B_k1], [B_mixed])
                        ACT(osq[:, :, 0:n], of32[:, :, 0:n], AF.Square, [B_of], [B_osq])

                    def t_epi(h):
                        ps, Bp = nps()
                        MM(ps[:, 0:n], ones_bf[:], osq[:, h, 0:n], True, True, [B_osq, B_c], Bp)
                        ACT(rst2[:, 0:n], ps[:, 0:n], AF.Ln, [Bp], [B_rst2], scale=1.0 / 128.0, bias=EPS)
                        ACT(rst2[:, 0:n], rst2[:, 0:n], AF.Exp, [B_rst2], [B_rst2], scale=-0.5)
                        STT(t1[:, 0:n], of32[:, h, 0:n], glag[:, h:h + 1], rst2[:, 0:n], ALU.mult, ALU.mult,
                            [B_of, B_k1, B_rst2], [B_t1])
                        TT_('dve', mixedT[:, 4 + h, 0:n], t1[:, 0:n], sg2[pb][:, h, 0:n], ALU.mult, [B_t1, B_sg2[pb]], [B_mixed])

                    def t_wout(ct):
                        ps, Bp = proj_fm(wout, B_wout, ct * 128, mixedT[:, :, 0:n], B_mixed, n)
                        add_into_x(ps, Bp, ct, t0, n)

                    return [t_pool] + [lambda h=h: t_epi(h) for h in range(4)] + [lambda ct=ct: t_wout(ct) for ct in range(8)]

                ps_lo[0], ps_n[0] = 0, 4
                for f in front_steps(0):
                    f()
                for bi in range(NBLK):
                    Ls = gla_steps(bi) + tail_steps(bi)
                    Ds = front_steps(bi + 1) if bi + 1 < NBLK else []
                    i = j = 0
                    while i < len(Ls) or j < len(Ds):
                        if j < len(Ds):
                            ps_lo[0], ps_n[0] = 0, 4
                            Ds[j]()
                            j += 1
                        if i < len(Ls):
                            ps_lo[0], ps_n[0] = 4, 4
                            Ls[i]()
                            i += 1
                ps_lo[0], ps_n[0] = 0, 6
                DMA('sp', gla_p.rearrange("(hp hr) k v -> (hr k) hp v", hr=2), Sf[:, :, :], [B_Sf], [], is_output=True)
                S.barrier()
        cut(13)
        with ExitStack() as es2:
            wqm = sb(es2, "wqm", [128, 8, 1024], BF16)
            wom = sb(es2, "wom", [128, 8, 1024], BF16)
            sel4 = sb(es2, "sel4", [4, 8], BF16)
            ohs = sb(es2, "ohs", [16, 16, 128], BF16)
            B_wom, B_k2 = [Buf() for _ in range(4)], Buf("k2")
            sq = [sb(es2, f"sq2_{i}", [128, 512], BF16) for i in range(2)]
            rstd = sb(es2, "rstd2", [128, 512], F32)
            B_sq, B_rstd = [Buf(), Buf()], Buf()
            ntmp = (sq, B_sq, rstd, B_rstd)
            stg = [sb(es2, f"stg2_{i}", [128, 2048], F32) for i in range(2)]
            B_stg = [Buf(), Buf()]
            B_wqm = [Buf(f"wqm{g}") for g in range(4)]
            DMA('sp', stg[0][0:4, 0:8], c_sel4, [], [B_stg[0]])
            CP('dve', sel4[:, :], stg[0][0:4, 0:8], [B_stg[0]], [B_k2])
            DMA('sp', stg[1][0:16, :], c_e[:, 0].rearrange("a s m -> a (s m)"), [], [B_stg[1]])
            CP('dve', ohs[:].rearrange("a s m -> a (s m)"), stg[1][0:16, :], [B_stg[1]], [B_k2])
            wqm_v = w_qm.rearrange("(c p) f -> p c f", p=128)
            stage_rr[0] = 0
            for g in range(4):
                load_w((stg, B_stg), wqm[:, :, g * 256:(g + 1) * 256], wqm_v[:, :, g * 256:(g + 1) * 256], B_wqm[g], 256, 8)
            load_w((stg, B_stg), wom, w_om.rearrange("(c p) f -> p c f", p=128), B_wom, 1024, 8)
            cut(20)

            def s2_tail(t0, n, ctx3, B_ctx):
                for ct in range(8):
                    ps, Bp = proj_fm(wom, B_wom, ct * 128, ctx3, B_ctx, n)
                    add_into_x(ps, Bp, ct, t0, n)

            with ExitStack() as esa:
                hms = sb(esa, "hms", [128, 8, NS], BF16)
                qs_bf = sb(esa, "qs_bf", [16, 1024], BF16)
                qm = [sb(esa, f"qm{i}", [16, 1024], BF16) for i in range(2)]
                qb = sb(esa, "qb", [128, 1024], F32)
                Kt = [sb(esa, f"Kt{i}", [128, 2, 1024], F32) for i in range(2)]
                Vb = [sb(esa, f"Vb{i}", [128, 2, 1024], BF16) for i in range(2)]
                junk = sb(esa, "junk", [128, 256], F32)
                SC = sb(esa, "SC", [128, NS, 2, 4], F32)
                Ebf = sb(esa, "Ebf", [128, NS, 2, 4], BF16)
                rs = sb(esa, "rs", [4, NS], F32)
                ctxs = [sb(esa, f"ctxs{i}", [4, 1024], BF16) for i in range(2)]
                ctxTs = sb(esa, "ctxTs", [128, 8, NS], BF16)
                B_hms, B_qs, B_qm, B_qb, B_Kt, B_Vt, B_Vb = Buf(), Buf(), [Buf(), Buf()], Buf(), [Buf(), Buf()], [Buf(), Buf()], [Buf(), Buf()]
                B_junk, B_SC, B_E, B_rs, B_ctxs, B_ctxTs = Buf(), [Buf() for _ in range(NS)], [Buf() for _ in range(NS)], [Buf() for _ in range(NS)], [Buf(), Buf()], Buf()
                NB = 512
                hm = sb(esa, "hm", [128, 8, NB], BF16)
                qmT = sb(esa, "qmT", [128, 8, NB], BF16)
                ex = [sb(esa, f"ex{i}", [128, 2, NB], BF16) for i in range(2)]
                rinv = sb(esa, "rinv", [128, NB], F32)
                ctxT = sb(esa, "ctxT", [128, 8, NB], BF16)
                B_hm, B_qmT, B_ex, B_rinv, B_ctxT = Buf(), Buf(), [Buf(), Buf()], Buf(), Buf()
                psM, BpM = PSR[0], B_PSR[0]
                psT, BpT = PSR[1], B_PSR[1]


                def s_dma(s):
                    i = s % 2
                    DMA('sp', Kt[i][:, :, :], ck[s].rearrange("(mt p) f -> p mt f", p=128), [], [B_Kt[i]])

                def s_dma_v(s):
                    i = s % 2
                    DMA('pool', Vb[i][:, :, :], cv[s].rearrange("(mt p) f -> p mt f", p=128), [], [B_Vb[i]])

                def s_qm(s):
                    i = s % 2
                    ACT(qm[i][:, :], qs_bf[:, :], AF.Copy, [B_qs, B_c], [B_qm[i]], scale=ident[0:16, s:s + 1])

                def s_bcast(s):
                    for half in range(2):
                        ps, Bp = nps()
                        MM(ps[:, :], ohs[:, s, :], qs_bf[:, half * 512:(half + 1) * 512], True, True, [B_k2, B_qs], Bp)
                        CP('act', qb[:, half * 512:(half + 1) * 512], ps[:, :], [Bp], [B_qb])

                def s_scores(s):
                    i = s % 2
                    for mt in range(2):
                        for h in range(4):
                            S.add('dve', lambda e, i=i, mt=mt, h=h, s=s: e.scalar_tensor_tensor(
                                out=junk[:, :], in0=Kt[i][:, mt, h * 256:(h + 1) * 256], scalar=1.0 / 16.0,
                                in1=qb[:, h * 256:(h + 1) * 256], op0=ALU.mult, op1=ALU.mult,
                                accum_out=SC[:, s, mt, h:h + 1]),
                                reads=[B_Kt[i], B_qb], writes=[B_junk, B_SC[s]])

                def s_vcast_exp(s):
                    i = s % 2
                    ACT(Ebf[:, s].rearrange("p a h -> p (a h)"), SC[:, s].rearrange("p a h -> p (a h)"), AF.Exp,
                        [B_SC[s]], [B_E[s]])

                def s_sums(s):
                    for mt in range(2):
                        MM(psM[0:4, s:s + 1], Ebf[:, s, mt, :], ones_bf[:, 0:1], mt == 0, mt == 1, [B_E[s], B_c], BpM)
                    S.add('dve', lambda e, s=s: e.reciprocal(out=rs[:, s:s + 1], in_=psM[0:4, s:s + 1]), reads=[BpM],
                          writes=[B_rs[s]])

                def s_ctx(s):
                    i = s % 2
                    for half in range(2):
                        ps, Bp = nps()
                        for mt in range(2):
                            MM(ps[0:4, :], Ebf[:, s, mt, :], Vb[i][:, mt, half * 512:(half + 1) * 512], mt == 0, mt == 1,
                               [B_E[s], B_Vb[i]], Bp)
                        ACT(ctxs[i][:, half * 512:(half + 1) * 512], ps[0:4, :], AF.Copy, [Bp, B_rs[s]], [B_ctxs[i]],
                            scale=rs[0:4, s:s + 1])

                def s_sel(s):
                    i = s % 2
                    for c in range(8):
                        MM(psT[:, c * 16 + s:c * 16 + s + 1], ctxs[i][:, c * 128:(c + 1) * 128], sel4[:, c:c + 1], True, True,
                           [B_ctxs[i], B_k2], BpT)

                def ok(s):
                    return 0 <= s < NS

                s_dma(0)
                s_dma_v(0)

                def p_front(bi):
                    t0 = bi * NB
                    rmsnorm(ntmp, xT[:, :, t0:t0 + NB], xbufs(t0, NB), 2, hm, B_hm, NB)

                def p_qm(bi):
                    for c in range(8):
                        ps, Bp = proj_fm(wqm, B_wqm[c // 2], c * 128, hm, B_hm, NB)
                        CP(('act', 'dve')[c % 2], qmT[:, c, :], ps[:, 0:NB], [Bp], [B_qmT])

                def p_scores(h):
                    e_, Be = ex[h % 2], B_ex[h % 2]
                    for mt in range(2):
                        ps, Bp = nps()
                        for j in range(2):
                            MM(ps[:, 0:NB], mkT[:, 2 * h + j, mt * 128:(mt + 1) * 128], qmT[:, 2 * h + j, :], j == 0, j == 1,
                               [B_mkT, B_qmT], Bp)
                        ACT(e_[:, mt, :], ps[:, 0:NB], AF.Exp, [Bp], [Be], scale=1.0 / 16.0)

                def p_ctx(h):
                    e_, Be = ex[h % 2], B_ex[h % 2]
                    ps, Bp = nps()
                    for mt in range(2):
                        MM(ps[:, 0:NB], ones_bf[:], e_[:, mt, :], mt == 0, mt == 1, [B_c, Be], Bp)
                    ACT(rinv[:, :], ps[:, 0:NB], AF.Ln, [Bp], [B_rinv])
                    ACT(rinv[:, :], rinv[:, :], AF.Exp, [B_rinv], [B_rinv], scale=-1.0)
                    for j in range(2):
                        ps, Bp = nps()
                        for mt in range(2):
                            MM(ps[:, 0:NB], mvb[:, mt, (2 * h + j) * 128:(2 * h + j + 1) * 128], e_[:, mt, :], mt == 0, mt == 1,
                               [B_mvb, Be], Bp)
                        TT_('dve', ctxT[:, 2 * h + j, :], ps[:, 0:NB], rinv[:, :], ALU.mult, [Bp, B_rinv], [B_ctxT])

                it = 0
                nblk = T // NB
                p_front(0)
                p_qm(0)
                rmsnorm(ntmp, xT[:, :, T:T + NS], xbufs(T, NS), 2, hms[:, :, :], B_hms, NS)
                for half in range(2):
                    ps, Bp = proj_tm(wqm, B_wqm[2 * half:2 * half + 2], half * 512, 512, hms, B_hms, 0, NS)
                    CP('act', qs_bf[:, half * 512:(half + 1) * 512], ps[0:NS, :], [Bp], [B_qs])
                for bi in range(nblk):
                    p_scores(0)
                    for h in range(4):
                        if ok(it + 1):
                            s_dma(it + 1)
                        if ok(it):
                            s_bcast(it)
                        if h + 1 < 4:
                            p_scores(h + 1)
                        if ok(it - 1):
                            s_sums(it - 1)
                        if ok(it - 2):
                            s_sel(it - 2)
                        p_ctx(h)
                        if ok(it - 1):
                            s_ctx(it - 1)
                        if ok(it + 1):
                            s_dma_v(it + 1)
                        if ok(it):
                            s_scores(it)
                            s_vcast_exp(it)
                        it += 1
                        if h == 1 and bi + 1 < nblk:
                            p_front(bi + 1)
                    s2_tail(bi * NB, NB, ctxT, B_ctxT)
                    if bi + 1 < nblk:
                        p_qm(bi + 1)
                s_sums(NS - 1)
                s_sel(NS - 2)
                s_ctx(NS - 1)
                s_sel(NS - 1)
                CP('act', ctxTs[:, :, :], psT[:, 0:128].rearrange("p (c s) -> p c s", c=8), [BpT], [B_ctxTs])
                s2_tail(T, NS, ctxTs, B_ctxTs)
                S.barrier()
        blocks = [(b * 512, 512) for b in range(4)] + [(T, NS)]
        cut(22)
        with ExitStack() as es3:
            hfT = sb(es3, "hfT", [128, 8, TT], BF16)
            B_hf = [Buf(f"hf{u}") for u in range(9)]

            def units(t0, n):
                return list(range(t0 // 256, (t0 + n - 1) // 256 + 1))

            mblocks = [(0, 512), (512, 512), (1024, 512), (1536, 264), (1800, 264)]
            a2 = [sb(es3, f"a2_{i}", [128, 4, TT], BF16) for i in range(2)]
            B_a2 = [[Buf() for _ in range(9)] for _ in range(2)]
            wu = [sb(es3, f"wu{i}", [128, 8, 512], BF16) for i in range(2)]
            wd = [sb(es3, f"wd{i}", [128, 4, 1024], BF16) for i in range(2)]
            B_wu, B_wd = [[Buf(), Buf()] for _ in range(2)], [[Buf(), Buf()] for _ in range(2)]
            stg = [sb(es3, f"stg3_{i}", [128, 2048], F32) for i in range(2)]
            B_stg = [Buf(), Buf()]
            rr = [sb(es3, f"rr{i}", [128, 512], F32) for i in range(2)]
            B_rr = [Buf(), Buf()]
            sq = [sb(es3, f"sq3_{i}", [128, 512], BF16) for i in range(2)]
            rstd = sb(es3, "rstd3", [128, 512], F32)
            ntmp3 = (sq, [Buf(), Buf()], rstd, Buf())
            wup_v = w_up.rearrange("(c p) f -> p c f", p=128)
            wdn_v = w_down.rearrange("(ft p) d -> p ft d", p=128)
            NG = 8
            rrk = [0]

            def load_group(fg):
                j = fg % 2
                load_w((stg, B_stg), wu[j], wup_v[:, :, fg * 512:(fg + 1) * 512], B_wu[j], 512, 8)
                load_w((stg, B_stg), wd[j], wdn_v[:, fg * 4:(fg + 1) * 4, :], B_wd[j], 1024, 4)

            def up(fg):
                j = fg % 2
                for ft in range(4):
                    for (t0, n) in mblocks:
                        ps, Bp = proj_fm(wu[j], B_wu[j], ft * 128, hfT[:, :, t0:t0 + n], [B_hf[u] for u in units(t0, n)], n)
                        k = rrk[0] % 2
                        rrk[0] += 1
                        ACT(rr[k][:, 0:n], ps[:, 0:n], AF.Relu, [Bp], [B_rr[k]])
                        ACT(a2[j][:, ft, t0:t0 + n], rr[k][:, 0:n], AF.Square, [B_rr[k]], [B_a2[j][u] for u in units(t0, n)])

            def down(fg):
                j = fg % 2
                for ct in range(8):
                    for (t0, n) in mblocks:
                        ps, Bp = nps()
                        for ft in range(4):
                            MM(ps[:, 0:n], wd[j][:, ft, ct * 128:(ct + 1) * 128], a2[j][:, ft, t0:t0 + n], ft == 0, ft == 3,
                               B_wd[j] + [B_a2[j][u] for u in units(t0, n)], Bp)
                        add_into_x(ps, Bp, ct, t0, n)

            def down_block(fg, b):
                j = fg % 2
                t0, n = blocks[b]
                for ct in range(8):
                    ps, Bp = nps()
                    for ft in range(4):
                        MM(ps[:, 0:n], wd[j][:, ft, ct * 128:(ct + 1) * 128], a2[j][:, ft, t0:t0 + n], ft == 0, ft == 3,
                           B_wd[j] + [B_a2[j][u] for u in units(t0, n)], Bp)
                    add_into_x(ps, Bp, ct, t0, n)

            yTv = [stg[i][:, :].rearrange("p (c t) -> p c t", c=8) for i in range(2)]
            wuf = [wu[i][:].rearrange("p c f -> p (c f)").bitcast(F32) for i in range(2)]
            yrow = [wuf[0][:, 0:1024], wuf[0][:, 1024:2048], wuf[1][:, 0:1024], wuf[1][:, 1024:2048]]
            B_yrw = [B_wu[0][0], B_wu[0][1], B_wu[1][0], B_wu[1][1]]
            fin_k = [0]

            def fin_norm(hb, k):
                t0, nh = hb
                rmsnorm(ntmp3, xT[:, :, t0:t0 + nh], xbufs(t0, nh), 4, yTv[k % 2][:, :, 0:nh], B_stg[k % 2], nh)

            def fin_out(hb, k):
                t0, nh = hb
                tw = min(nh, 128)
                for tt in range(max(1, nh // 128)):
                    i = fin_k[0] % 4
                    fin_k[0] += 1
                    for half in range(2):
                        ps, Bp = nps()
                        for c4 in range(4):
                            TR(ps[0:tw, c4 * 128:(c4 + 1) * 128], yTv[k % 2][:, half * 4 + c4, tt * 128:tt * 128 + tw], ident[:, :],
                               [B_stg[k % 2], B_c], Bp)
                        CP(('act', 'dve')[half], yrow[i][0:tw, half * 512:(half + 1) * 512], ps[0:tw, :], [Bp], [B_yrw[i]])
                    if nh == NS:
                        DMA('sp', y_s, yrow[i][0:tw, :], [B_yrw[i]], [], is_output=True)
                    else:
                        r0 = t0 + tt * 128
                        DMA('sp', y_p[r0:r0 + 128, :], yrow[i][:, :], [B_yrw[i]], [], is_output=True)

            load_group(0)
            load_group(1)
            for b, (t0, n) in enumerate(blocks):
                rmsnorm(ntmp3, xT[:, :, t0:t0 + n], xbufs(t0, n), 3, hfT[:, :, t0:t0 + n], [B_hf[u] for u in units(t0, n)], n)
            up(0)
            for fg in range(NG):
                if fg + 1 < NG:
                    up(fg + 1)
                if fg < NG - 1:
                    down(fg)
                else:
                    order = (4, 0, 1, 2, 3)
                    halves = []
                    for b in order:
                        t0, n = blocks[b]
                        for h0 in range(0, n, 256):
                            halves.append((b, (t0 + h0, min(256, n - h0))))
                    down_block(fg, order[0])
                    done_b = {order[0]}
                    nxt = 1
                    for k, (b, hb) in enumerate(halves):
                        while nxt < len(order) and (b not in done_b or (k + 1 < len(halves) and halves[k + 1][0] not in done_b)):
                            down_block(fg, order[nxt])
                            done_b.add(order[nxt])
                            nxt += 1
                        fin_norm(hb, k)
                        if k >= 1:
                            fin_out(halves[k - 1][1], k - 1)
                    fin_out(halves[-1][1], len(halves) - 1)
                if fg + 2 < NG:
                    load_group(fg + 2)

        S.finish('sp')
        S.emit(sem_e, sem_d)
    return nc


_CACHE = {}


def _consts():
    ident = np.eye(128, dtype=np.float32)
    j = np.arange(128)[:, None]
    i = np.arange(128)[None, :]
    tri = (j <= i).astype(np.float32)
    trirev = (j > i).astype(np.float32)
    mask4 = np.tile(tri, (1, 4)).astype(np.float32)
    e = np.zeros((16, 3, 16, 128), np.float32)
    for s in range(16):
        e[s, 0, s, :] = 1.0
        e[s, 1, s, :64] = 1.0
        e[s, 2, s, 64:] = 1.0
    sel4 = np.zeros((4, 8), np.float32)
    for c in range(8):
        sel4[c // 2, c] = 1.0
    invcnt = np.zeros((128, 4, 16), np.float32)
    for g, w in enumerate((2, 4, 8, 16)):
        for t in range(16):
            invcnt[:, g, t] = 1.0 / min(t + 1, w)
    return dict(c_ident=ident, c_tri=tri, c_trirev=trirev, c_mask4=mask4, c_e=e, c_sel4=sel4, c_invcnt=invcnt)


def kernel(x_prompt, x_sample, state_pool, state_gla, cache_mem_k, cache_mem_v, mem_prompt,
           norm_mix_g, w_in, w_forget_up, b_forget, w_pool, pool_scale, gla_norm_g, w_out,
           mem_norm_g, w_km, w_vm, norm_mem_g, w_qm, w_om, norm_ffn_g, w_up, w_down, norm_final_g):
    f = lambda a: np.ascontiguousarray(np.asarray(a, dtype=np.float32))
    if 'nc' not in _CACHE:
        _CACHE['nc'] = build_nc()
    nc = _CACHE['nc']
    wfu = np.zeros((32, 256), np.float32)
    wfu[0:16] = np.asarray(w_forget_up)[0]
    wfu[16] = np.asarray(b_forget)[0]
    gains = np.stack([np.asarray(g, np.float32).reshape(8, 128).T for g in
                      (norm_mix_g[0], mem_norm_g[0], norm_mem_g[0], norm_ffn_g[0], norm_final_g)], axis=1)
    shared = dict(
        w_in=f(w_in[0]), w_fu=wfu,
        w_pool=f(np.transpose(np.asarray(w_pool)[0], (1, 0, 2))),
        pscale=f(np.asarray(pool_scale)[0].reshape(4, 128).T),
        gla_g=f(np.asarray(gla_norm_g)[0].T),
        w_out=f(w_out[0]), w_km=f(w_km[0]), w_vm=f(w_vm[0]), w_qm=f(w_qm[0]), w_om=f(w_om[0]),
        w_up=f(w_up[0]), w_down=f(w_down[0]), gains=f(gains), **_consts())
    in_maps = []
    for c in range(NCORES):
        s0, s1 = c * NS, (c + 1) * NS
        m = dict(shared)
        m.update(xp=f(x_prompt[c]), xs=f(np.asarray(x_sample)[s0:s1, 0, :]),
                 spool=f(np.asarray(state_pool)[0, s0:s1].reshape(NS * 15, 512)),
                 sgla=f(np.asarray(state_gla)[0, s0:s1]),
                 ck=f(np.asarray(cache_mem_k)[0, s0:s1].reshape(NS, 256, 1024)),
                 cv=f(np.asarray(cache_mem_v)[0, s0:s1].reshape(NS, 256, 1024)),
                 memp=f(mem_prompt[c]))
        in_maps.append(m)
    res = run_bass_kernel_spmd(nc, in_maps, core_ids=list(range(NCORES)), **({'trace': True} if os.environ.get('K_TRACE') else {}))
    if os.environ.get('K_TRACE'):
        print('EXEC_NS', res.exec_time_ns)
    R = res.results
    y_prompt = np.stack([R[c]["y_p"] for c in range(NCORES)])
    y_sample = np.concatenate([R[c]["y_s"] for c in range(NCORES)])[:, None, :]
    pool_p = np.stack([R[c]["pool_p"] for c in range(NCORES)])[None]
    gla_p = np.stack([R[c]["gla_p"] for c in range(NCORES)])[None]
    mk_p = np.stack([R[c]["mk_p"].reshape(256, 4, 256) for c in range(NCORES)])[None]
    mv_p = np.stack([R[c]["mv_p"].reshape(256, 4, 256) for c in range(NCORES)])[None]
    pool_s = np.concatenate([R[c]["pool_s"] for c in range(NCORES)])[None]
    gla_s = np.concatenate([R[c]["gla_s"] for c in range(NCORES)])[None]
    return (y_prompt.astype(np.float32), y_sample.astype(np.float32), pool_p.astype(np.float32),
            gla_p.astype(np.float32), mk_p.astype(np.float32), mv_p.astype(np.float32),
            pool_s.astype(np.float32), gla_s.astype(np.float32))
```
